# Optimizing a Trainium2 kernel written in Bass

```python
import math
import jax, jax.numpy as jnp
from jax import lax
import numpy as np

D_MODEL = 4096
BATCH = 2
SEQ = 4096
DEPTH = 2

CTX_LEN = 256
GRID_W = 64
HEAD_DIM = 128
AXIS_DIM = HEAD_DIM // 2
ROPE_THETA = 10000.0
EPS = 1e-6
Q_BLOCK = 128
ATTN_SCALE = 1.0 / math.sqrt(HEAD_DIM)

A_WIDTH = D_MODEL // 2
A_HEADS = A_WIDTH // HEAD_DIM
A_KV_HEADS = A_HEADS // 4
A_GROUP = A_HEADS // A_KV_HEADS
B_WIDTH = D_MODEL // 4
CONV_K = 3
C_WIDTH = D_MODEL // 4
C_VDIM = 2 * HEAD_DIM
C_HEADS = C_WIDTH // C_VDIM
MIX_WIDTH = A_WIDTH + B_WIDTH + C_WIDTH

KA_COLS = A_KV_HEADS * HEAD_DIM
VA_COLS = A_KV_HEADS * HEAD_DIM
KC_COLS = C_HEADS * 2 * HEAD_DIM
VC_COLS = C_WIDTH
KV_COLS = KA_COLS + VA_COLS + KC_COLS + VC_COLS
QA_COLS = A_WIDTH
QC_COLS = C_HEADS * 2 * HEAD_DIM
REST_COLS = QA_COLS + QC_COLS + 3 * B_WIDTH + A_WIDTH + B_WIDTH + C_WIDTH
IN_COLS = KV_COLS + REST_COLS

kernel_name = "hybrid_gqa_shortconv_diffattn_prefix_dit"


def _split(u, sizes):
    idx, acc = [], 0
    for s in sizes[:-1]:
        acc += s
        idx.append(acc)
    return jnp.split(u, idx, axis=-1)


def _rms(x, g):
    xf = x.astype(jnp.float32)
    y = xf * lax.rsqrt(jnp.mean(xf * xf, axis=-1, keepdims=True) + EPS)
    return y.astype(x.dtype) * g


def _modulation(cvec, w_mod, b_mod):
    m = jax.nn.silu(cvec) @ w_mod + b_mod
    return jnp.split(m, 3, axis=-1)


def _axial_rope_tables(n):
    rows = n // GRID_W
    row = jnp.broadcast_to(jnp.arange(rows)[:, None], (rows, GRID_W)).reshape(-1)
    col = jnp.broadcast_to(jnp.arange(GRID_W)[None, :], (rows, GRID_W)).reshape(-1)
    inv = ROPE_THETA ** (-jnp.arange(0, AXIS_DIM, 2, dtype=jnp.float32) / AXIS_DIM)
    ar = row.astype(jnp.float32)[:, None] * inv
    ac = col.astype(jnp.float32)[:, None] * inv
    return (jnp.cos(ar), jnp.sin(ar), jnp.cos(ac), jnp.sin(ac))


def _rot(xp, cos, sin):
    x1, x2 = jnp.split(xp, 2, axis=-1)
    return jnp.concatenate([x1 * cos - x2 * sin, x2 * cos + x1 * sin], axis=-1)


def _apply_rope(x, tabs):
    shp = (x.shape[1],) + (1,) * (x.ndim - 3) + (AXIS_DIM // 2,)
    cr, sr, cc, sc = [t.reshape(shp).astype(x.dtype) for t in tabs]
    xr, xc = x[..., :AXIS_DIM], x[..., AXIS_DIM:]
    return jnp.concatenate([_rot(xr, cr, sr), _rot(xc, cc, sc)], axis=-1)


def _conv3(x, w):
    xp = jnp.pad(x, ((0, 0), (1, 1), (0, 0)))
    return xp[:, :-2] * w[0] + xp[:, 1:-1] * w[1] + xp[:, 2:] * w[2]


def _attend_gqa(q, k, v):
    s = jnp.einsum('bqkgd,bskd->bkgqs', q, k, preferred_element_type=jnp.float32) * ATTN_SCALE
    p = jax.nn.softmax(s, axis=-1).astype(v.dtype)
    return jnp.einsum('bkgqs,bskd->bqkgd', p, v)


def _attend_diff(q, k, v, lam):
    s = jnp.einsum('bqhcd,bshcd->bhcqs', q, k, preferred_element_type=jnp.float32) * ATTN_SCALE
    p = jax.nn.softmax(s, axis=-1)
    pd = (p[:, :, 0] - lam.astype(jnp.float32) * p[:, :, 1]).astype(v.dtype)
    return jnp.einsum('bhqs,bshe->bqhe', pd, v)


def _sweep_blocks(fn, q):
    b, s = q.shape[:2]
    nb = s // Q_BLOCK
    qb = jnp.moveaxis(q.reshape((b, nb, Q_BLOCK) + q.shape[2:]), 1, 0)
    ob = lax.map(fn, qb)
    return jnp.moveaxis(ob, 0, 1).reshape((b, s) + ob.shape[3:])


def _prep_kv(ukv, k_norm_a, tabs):
    b, t = ukv.shape[:2]
    ka, va, kc, vc = _split(ukv, [KA_COLS, VA_COLS, KC_COLS, VC_COLS])
    ka = _rms(ka.reshape(b, t, A_KV_HEADS, HEAD_DIM), k_norm_a)
    va = va.reshape(b, t, A_KV_HEADS, HEAD_DIM)
    kc = kc.reshape(b, t, C_HEADS, 2, HEAD_DIM)
    vc = vc.reshape(b, t, C_HEADS, C_VDIM)
    if tabs is not None:
        ka = _apply_rope(ka, tabs)
        kc = _apply_rope(kc, tabs)
    return ka, va, kc, vc


def _mix(urest, ka, va, kc, vc, q_norm_a, conv_w, lam, subln_g, lambda_init, w_out, tabs):
    b, t = urest.shape[:2]
    qa, qc, xb, bb, cb, za, zb, zc = _split(
        urest, [QA_COLS, QC_COLS, B_WIDTH, B_WIDTH, B_WIDTH, A_WIDTH, B_WIDTH, C_WIDTH])
    qa = _rms(qa.reshape(b, t, A_KV_HEADS, A_GROUP, HEAD_DIM), q_norm_a)
    qc = qc.reshape(b, t, C_HEADS, 2, HEAD_DIM)
    fa = lambda qblk: _attend_gqa(qblk, ka, va)
    fc = lambda qblk: _attend_diff(qblk, kc, vc, lam)
    if tabs is not None:
        qa = _apply_rope(qa, tabs)
        qc = _apply_rope(qc, tabs)
        oa = _sweep_blocks(fa, qa)
        oc = _sweep_blocks(fc, qc)
    else:
        oa = fa(qa)
        oc = fc(qc)
    oa = oa.reshape(b, t, A_WIDTH) * jax.nn.silu(za)
    yb = bb * _conv3(cb * xb, conv_w) * jax.nn.silu(zb)
    oc = (_rms(oc, subln_g) * (1.0 - lambda_init)).reshape(b, t, C_WIDTH) * jax.nn.silu(zc)
    return jnp.concatenate([oa, yb, oc], axis=-1) @ w_out


def setup_inputs(seed: int = 0) -> dict:
    key = jax.random.key(seed)
    ks = jax.random.split(key, 20)
    f32 = jnp.float32
    nrm = lambda k, shp: jax.random.normal(k, shp, f32)
    return {
        "x": nrm(ks[0], (BATCH, SEQ, D_MODEL)),
        "c": nrm(ks[1], (BATCH, D_MODEL)),
        "ctx": nrm(ks[2], (BATCH, CTX_LEN, D_MODEL)),
        "c_ctx": nrm(ks[3], (D_MODEL,)),
        "w_mod": nrm(ks[4], (DEPTH, D_MODEL, 3 * D_MODEL)) * (0.5 * D_MODEL ** -0.5),
        "b_mod": nrm(ks[5], (DEPTH, 3 * D_MODEL)) * 0.01,
        "norm_g": 1.0 + 0.02 * nrm(ks[6], (DEPTH, D_MODEL)),
        "w_in": nrm(ks[7], (DEPTH, D_MODEL, IN_COLS)) * D_MODEL ** -0.5,
        "q_norm_a": 1.0 + 0.02 * nrm(ks[8], (DEPTH, HEAD_DIM)),
        "k_norm_a": 1.0 + 0.02 * nrm(ks[9], (DEPTH, HEAD_DIM)),
        "conv_w": nrm(ks[10], (DEPTH, CONV_K, B_WIDTH)) * CONV_K ** -0.5,
        "lambda_q1": 0.1 * nrm(ks[11], (DEPTH, HEAD_DIM)),
        "lambda_k1": 0.1 * nrm(ks[12], (DEPTH, HEAD_DIM)),
        "lambda_q2": 0.1 * nrm(ks[13], (DEPTH, HEAD_DIM)),
        "lambda_k2": 0.1 * nrm(ks[14], (DEPTH, HEAD_DIM)),
        "subln_g": 1.0 + 0.02 * nrm(ks[15], (DEPTH, C_VDIM)),
        "w_out": nrm(ks[16], (DEPTH, MIX_WIDTH, D_MODEL)) * MIX_WIDTH ** -0.5,
        "final_g": 1.0 + 0.02 * nrm(ks[17], (D_MODEL,)),
    }


def reference(x, c, ctx, c_ctx, w_mod, b_mod, norm_g, w_in, q_norm_a, k_norm_a, conv_w,
              lambda_q1, lambda_k1, lambda_q2, lambda_k2, subln_g, w_out, final_g):
    tabs = _axial_rope_tables(x.shape[1])
    h, hc = x, ctx
    for i in range(DEPTH):
        update_ctx = i < DEPTH - 1
        lambda_init = 0.8 - 0.6 * math.exp(-0.3 * i)
        lam = (jnp.exp(jnp.sum(lambda_q1[i] * lambda_k1[i])) -
               jnp.exp(jnp.sum(lambda_q2[i] * lambda_k2[i])) + lambda_init)
        shift, scale, gate = [m[:, None, :] for m in _modulation(c, w_mod[i], b_mod[i])]
        cshift, cscale, cgate = _modulation(c_ctx, w_mod[i], b_mod[i])
        n = _rms(h, norm_g[i]) * (1.0 + scale) + shift
        nc = _rms(hc, norm_g[i]) * (1.0 + cscale) + cshift
        u = n @ w_in[i]
        uc = nc @ (w_in[i] if update_ctx else w_in[i][:, :KV_COLS])
        kvl = _prep_kv(u[..., :KV_COLS], k_norm_a[i], tabs)
        kvc = _prep_kv(uc[..., :KV_COLS], k_norm_a[i], None)
        ka, va, kc, vc = [jnp.concatenate([a, b_], axis=1) for a, b_ in zip(kvc, kvl)]
        out = _mix(u[..., KV_COLS:], ka, va, kc, vc, q_norm_a[i], conv_w[i], lam, subln_g[i],
                   lambda_init, w_out[i], tabs)
        if update_ctx:
            out_c = _mix(uc[..., KV_COLS:], *kvc, q_norm_a[i], conv_w[i], lam, subln_g[i],
                         lambda_init, w_out[i], None)
            hc = hc + cgate * out_c
        h = h + gate * out
    return _rms(h, final_g)
```

```python
import math
import numpy as np
import concourse.bass as bass
import concourse.mybir as mybir
from concourse.bass_utils import run_bass_kernel_spmd

F32 = mybir.dt.float32
BF16 = mybir.dt.bfloat16
AF = mybir.ActivationFunctionType
ALU = mybir.AluOpType
AX = mybir.AxisListType

N_CORES = 8
RANKS = 4


class Cfg:
    def __init__(self, d_model=4096, seq=4096, ctx_len=256, depth=2, batch=2):
        self.D = d_model
        self.SEQ = seq
        self.CTX = ctx_len
        self.DEPTH = depth
        self.BATCH = batch
        self.KC = d_model // 128
        self.TOK = seq // RANKS
        self.HD = 128
        self.A_W = d_model // 2
        self.A_H = self.A_W // 128
        self.A_KV = self.A_H // 4
        self.B_W = d_model // 4
        self.C_W = d_model // 4
        self.C_H = self.C_W // 256
        self.KA = self.A_KV * 128
        self.VA = self.A_KV * 128
        self.KCC = self.C_H * 256
        self.VC = self.C_W
        self.KV = self.KA + self.VA + self.KCC + self.VC
        self.IN = self.KV + self.A_W + self.C_H * 256 + 3 * self.B_W + self.A_W + self.B_W + self.C_W
        self.MODC = 3 * d_model // RANKS
        self.WT = min(512, self.KA)


def build_mod(cfg):
    nc = bass.Bass("TRN2", target_bir_lowering=False)
    KC, MODC, L = cfg.KC, cfg.MODC, cfg.DEPTH
    WT = 512 if MODC % 512 == 0 else 256
    NT = MODC // WT
    cT = nc.dram_tensor("cT", [128, KC, 2], F32, kind="ExternalInput").ap()
    wm = nc.dram_tensor("wm", [L, cfg.D, MODC], F32, kind="ExternalInput").ap()
    bm = nc.dram_tensor("bm", [L, MODC], F32, kind="ExternalInput").ap()
    mo = nc.dram_tensor("mo", [L, 2, MODC], F32, kind="ExternalOutput").ap()
    with (
        nc.sbuf_tensor("c32", [128, KC, 2], F32) as c32,
        nc.sbuf_tensor("cb", [128, KC, 2], BF16) as cb,
        nc.sbuf_tensor("w0", [128, KC, WT], BF16) as w0,
        nc.sbuf_tensor("w1", [128, KC, WT], BF16) as w1,
        nc.sbuf_tensor("bb", [2, L, MODC], F32) as bb,
        nc.sbuf_tensor("res", [2, L, MODC], F32) as res,
        nc.psum_tensor("p0", [2, WT], F32) as p0,
        nc.psum_tensor("p1", [2, WT], F32) as p1,
        nc.semaphore("s_in") as s_in,
        nc.semaphore("s_act") as s_act,
        nc.semaphore("s_w0") as s_w0,
        nc.semaphore("s_w1") as s_w1,
        nc.semaphore("s_pe") as s_pe,
        nc.semaphore("s_ev") as s_ev,
        nc.semaphore("s_out") as s_out,
        nc.Block() as block,
    ):
        ws = [w0, w1]
        ps = [p0, p1]
        s_w = [s_w0, s_w1]
        tiles = [(l, t) for l in range(L) for t in range(NT)]

        @block.sync
        def _(s):
            s.dma_start(out=c32[:], in_=cT[:, :, :]).then_inc(s_in, 16)
            for r in range(2):
                s.dma_start(out=bb[r:r + 1, :, :], in_=bm.rearrange("(o l) m -> o l m", o=1)).then_inc(s_in, 16)

        @block.scalar
        def _(a):
            a.wait_ge(s_in, 48)
            a.activation(out=cb[:], in_=c32[:], func=AF.Silu).then_inc(s_act, 1)

        @block.gpsimd
        def _(g):
            for i, (l, t) in enumerate(tiles):
                if i >= 2:
                    g.wait_ge(s_pe, i - 1)
                g.dma_start(
                    out=ws[i % 2][:],
                    in_=wm[l].rearrange("(kc p) n -> p kc n", p=128)[:, :, t * WT:(t + 1) * WT],
                ).then_inc(s_w[i % 2], 16)
            g.wait_ge(s_ev, len(tiles))
            for l in range(L):
                g.dma_start(out=mo[l], in_=res[:, l, :]).then_inc(s_out, 16)
            g.wait_ge(s_out, 16 * L)

        @block.tensor
        def _(pe):
            pe.wait_ge(s_act, 1)
            for i, (l, t) in enumerate(tiles):
                pe.wait_ge(s_w[i % 2], 16 * (i // 2 + 1))
                if i >= 2:
                    pe.wait_ge(s_ev, i - 1)
                for k in range(KC):
                    ins = pe.matmul(ps[i % 2][:], lhsT=cb[:, k, :], rhs=ws[i % 2][:, k, :],
                                    start=(k == 0), stop=(k == KC - 1))
                ins.then_inc(s_pe, 1)

        @block.vector
        def _(v):
            v.wait_ge(s_in, 48)
            for i, (l, t) in enumerate(tiles):
                v.wait_ge(s_pe, i + 1)
                v.tensor_tensor(out=res[:, l, t * WT:(t + 1) * WT], in0=ps[i % 2][:],
                                in1=bb[:, l, t * WT:(t + 1) * WT], op=ALU.add).then_inc(s_ev, 1)
    return nc


def run_mod(cfg, c, c_ctx, w_mod, b_mod):
    nc = build_mod(cfg)
    in_maps = []
    for r in range(N_CORES):
        b, j = divmod(r, RANKS)
        rows = np.stack([c[b], c_ctx], axis=-1)
        cT = np.ascontiguousarray(rows.reshape(cfg.KC, 128, 2).transpose(1, 0, 2))
        sl = slice(j * cfg.MODC, (j + 1) * cfg.MODC)
        in_maps.append({
            "cT": cT.astype(np.float32),
            "wm": np.ascontiguousarray(w_mod[:, :, sl]),
            "bm": np.ascontiguousarray(b_mod[:, sl]),
        })
    res = run_bass_kernel_spmd(nc, in_maps, core_ids=list(range(N_CORES)))
    out = []
    for b in range(cfg.BATCH):
        out.append(np.concatenate([res.results[b * RANKS + j]["mo"] for j in range(RANKS)], axis=-1))
    return out


class Tok:
    __slots__ = ("sem", "val", "eng")

    def __init__(self, sem, val, eng):
        self.sem, self.val, self.eng = sem, val, eng


class Buf:
    __slots__ = ("name", "w", "r")

    def __init__(self, name):
        self.name, self.w, self.r = name, None, []


class Prog:
    ENGS = ("tensor", "vector", "scalar", "gpsimd", "sync")

    def __init__(self, nc, stack):
        self.nc, self.stack = nc, stack
        self.sem = {e: stack.enter_context(nc.semaphore("sem_" + e)) for e in self.ENGS[:4]}
        self.cnt = {e: 0 for e in self.ENGS[:4]}
        self.pending = {e: [] for e in self.ENGS}
        self.waited = {e: {} for e in self.ENGS}
        self.chans = {}
        self.nops = 0

    def chan(self, name):
        if name not in self.chans:
            self.chans[name] = [self.stack.enter_context(self.nc.semaphore("ch_" + name)), 0]
        return name

    def _waits(self, eng, deps):
        out = []
        w = self.waited[eng]
        for t in deps:
            if t is None:
                continue
            key = id(t.sem)
            if w.get(key, 0) < t.val:
                w[key] = t.val
                out.append((t.sem, t.val))
        return out

    def _deps(self, eng, reads, writes, is_dma):
        deps = []
        for b in reads:
            deps.append(b.w)
        for b in writes:
            if b.w is not None and (is_dma or b.w.eng != eng):
                deps.append(b.w)
            for t in b.r:
                if is_dma or t.eng != eng:
                    deps.append(t)
        return deps

    def _commit(self, tok, reads, writes):
        for b in reads:
            b.r.append(tok)
        for b in writes:
            b.w, b.r = tok, []

    def op(self, eng, fn, reads=(), writes=()):
        waits = self._waits(eng, self._deps(eng, reads, writes, False))
        self.cnt[eng] += 1
        tok = Tok(self.sem[eng], self.cnt[eng], eng)
        self.pending[eng].append((waits, fn, (self.sem[eng], 1)))
        self._commit(tok, reads, writes)
        self.nops += 1
        return tok

    def dma(self, queue, fn, chan, reads=(), writes=()):
        waits = self._waits(queue, self._deps(queue, reads, writes, True))
        c = self.chans[chan]
        c[1] += 16
        tok = Tok(c[0], c[1], None)
        self.pending[queue].append((waits, fn, (c[0], 16)))
        self._commit(tok, reads, writes)
        self.nops += 1
        return tok

    SCOPES = False

    def emit(self, scope=None):
        if scope and Prog.SCOPES:
            self.nscope = getattr(self, "nscope", 0) + 1
            with self.nc.named_scope("%03d_%s" % (self.nscope, scope)):
                return self.emit()
        finals = [(self.sem[e], self.cnt[e]) for e in self.ENGS[:4] if self.cnt[e] > 0]
        finals += [(c[0], c[1]) for c in self.chans.values() if c[1] > 0]

        def runner(eng, ops):
            def run(e):
                for waits, fn, inc in ops:
                    for s, v in waits:
                        e.wait_ge(s, v)
                    ins = fn(e)
                    ins.then_inc(inc[0], inc[1])
                w = self.waited[eng]
                for s, v in finals:
                    if w.get(id(s), 0) < v and not (eng in self.sem and s is self.sem[eng]):
                        e.wait_ge(s, v)
                        w[id(s)] = v
            return run

        with self.nc.Block() as block:
            for eng in self.ENGS:
                ops = self.pending[eng]
                getattr(block, eng)(runner(eng, ops))
                self.pending[eng] = []


def _bc(base, dims):
    return bass.AP(base.tensor, base.offset, [list(base.ap[0])] + [list(d) for d in dims])


EPS = 1e-6
ATTN_SCALE = 1.0 / math.sqrt(128.0)


class Layer:
    UID = 0

    def __init__(self, cfg, nc, stack, P, GT):
        self.cfg, self.nc, self.st, self.P, self.GT = cfg, nc, stack, P, GT
        self.uid = 0

    def sb(self, shape, dt, stack=None, name=None):
        Layer.UID += 1
        return (stack or self.st).enter_context(
            self.nc.sbuf_tensor("%s_%d" % (name or "t", Layer.UID), list(shape), dt))

    def ps(self, shape, dt, stack=None, name=None):
        Layer.UID += 1
        full = [128, 512] if dt == F32 else [128, 1024]
        assert shape[0] <= 128 and shape[1] <= full[1]
        return (stack or self.st).enter_context(
            self.nc.psum_tensor("%s_%d" % (name or "p", Layer.UID), full, dt))

    def load_consts(self, dram):
        cfg, P, nc = self.cfg, self.P, self.nc
        KC = cfg.KC
        P.chan("const")
        self.ident32 = self.sb([128, 128], F32, name="ident32")
        self.identb = self.sb([128, 128], BF16, name="identb")
        self.mcol = self.sb([128, 2, 3, KC], F32, name="mcol")
        self.ngcol = self.sb([128, KC], F32, name="ngcol")
        self.gscol = self.sb([128, 2, KC], F32, name="gscol")
        self.mhalf = self.sb([128, 8], F32, name="mhalf")
        self.cs = self.sb([128, self.GT // 128, 64], F32, name="cs")
        self.sn = self.sb([128, self.GT // 128, 64], F32, name="sn")
        self.dram = dram
        P.chan("rope")
        self.b_const = Buf("const")
        bc = self.b_const
        loads = [
            (self.ident32[:], dram["ident"][:, :]),
            (self.mcol[:], dram["mcol"]),
            (self.ngcol[:], dram["ng_col"]),
        ]
        for o, i in loads:
            P.dma("sync", (lambda o=o, i=i: lambda e: e.dma_start(out=o, in_=i))(), "const", writes=[bc])
        self.extra_const_loads(dram, bc)
        P.op("gpsimd", lambda e: e.memset(self.mhalf[:], -0.5), writes=[bc])
        P.op("vector", lambda e: e.tensor_copy(out=self.identb[:], in_=self.ident32[:]), reads=[bc], writes=[bc])
        for r in range(2):
            P.op("vector", (lambda r=r: lambda e: e.scalar_tensor_tensor(
                out=self.gscol[:, r, :], in0=self.mcol[:, r, 1, :], scalar=1.0, in1=self.ngcol[:],
                op0=ALU.add, op1=ALU.mult))(), reads=[bc], writes=[bc])

    def extra_const_loads(self, dram, bc):
        pass

    def load_rope(self, row0, ntiles):
        for t, nm in ((self.cs, "cs"), (self.sn, "sn")):
            src = self.dram[nm][row0:row0 + 128 * ntiles, :].rearrange("(t p) c -> p t c", p=128)
            self.P.dma("sync", (lambda t=t, src=src: lambda e: e.dma_start(out=t[:, 0:ntiles, :], in_=src))(),
                       "rope", writes=[self.b_const])

    def alloc_norm(self, stack):
        cfg = self.cfg
        self.hbuf = [self.sb([128, cfg.D], F32, stack, "hbuf") for _ in range(2)]
        self.b_h = [Buf("h0"), Buf("h1")]
        self.junk = self.sb([128, min(cfg.D, 2048)], F32, stack, "junk")
        self.b_junk = Buf("junk")
        self.stat = [self.sb([128, 8], F32, stack, "stat") for _ in range(2)]
        self.b_stat = [Buf("st0"), Buf("st1")]
        self.psT = [self.ps([128, 512], F32, stack, "psT") for _ in range(2)]
        self.b_psT = [Buf("psT0"), Buf("psT1")]
        self.P.chan("h0"); self.P.chan("h1")
        self.ncount = 0

    def norm_tiles(self, src, row0, nrows_list, row, nT, nT_bufs, col0=0, preloaded=False):
        cfg, P = self.cfg, self.P
        D, KC = cfg.D, cfg.KC
        CW = min(D, 2048)
        NCH = D // CW
        r0 = row0
        c0 = col0
        for ti, n in enumerate(nrows_list):
            s = self.ncount % 2
            self.ncount += 1
            hb, bh, stt, bst = self.hbuf[s], self.b_h[s], self.stat[s], self.b_stat[s]
            if not preloaded:
                P.dma("sync", (lambda hb=hb, r0=r0, n=n: lambda e: e.dma_start(out=hb[0:n, :], in_=src[r0:r0 + n, :]))(),
                      "h%d" % s, writes=[bh])
            for c in range(NCH):
                P.op("vector", (lambda hb=hb, c=c, n=n: lambda e: e.tensor_tensor(
                    out=self.junk[0:n, 0:CW], in0=hb[0:n, c * CW:(c + 1) * CW],
                    in1=hb[0:n, c * CW:(c + 1) * CW], op=ALU.mult))(), reads=[bh], writes=[self.b_junk])
                P.op("vector", (lambda stt=stt, c=c, n=n: lambda e: e.tensor_reduce(
                    out=stt[0:n, c:c + 1], in_=self.junk[0:n, 0:CW], axis=AX.X, op=ALU.add))(),
                    reads=[self.b_junk], writes=[bst])
            if NCH > 1:
                P.op("vector", (lambda stt=stt, n=n: lambda e: e.tensor_reduce(
                    out=stt[0:n, 4:5], in_=stt[0:n, 0:NCH], axis=AX.X, op=ALU.add))(), reads=[bst], writes=[bst])
                sscol = 4
            else:
                sscol = 0
            P.op("vector", (lambda stt=stt, n=n, sscol=sscol: lambda e: e.tensor_scalar(
                out=stt[0:n, 5:6], in0=stt[0:n, sscol:sscol + 1], scalar1=1.0 / D, scalar2=EPS,
                op0=ALU.mult, op1=ALU.add))(), reads=[bst], writes=[bst])
            P.op("gpsimd", (lambda stt=stt, n=n: lambda e: e.tensor_tensor(
                out=stt[0:n, 6:7], in0=stt[0:n, 5:6], in1=self.mhalf[0:n, 0:1], op=ALU.pow))(),
                reads=[bst, self.b_const], writes=[bst])
            P.op("scalar", (lambda hb=hb, stt=stt, n=n: lambda e: e.activation(
                out=hb[0:n, :], in_=hb[0:n, :], func=AF.Copy, scale=stt[0:n, 6:7]))(),
                reads=[bh, bst], writes=[bh])
            for kq in range((KC + 3) // 4):
                pb = kq % 2
                ks = list(range(kq * 4, min(KC, kq * 4 + 4)))

                def tr(e, hb=hb, ks=ks, pb=pb, n=n):
                    for q, k in enumerate(ks):
                        ins = e.transpose(out=self.psT[pb][:, q * 128:q * 128 + n],
                                          in_=hb[0:n, k * 128:(k + 1) * 128],
                                          identity=self.ident32[0:n, 0:n])
                    return ins
                P.op("tensor", tr, reads=[bh, self.b_const], writes=[self.b_psT[pb]])
                on_act = (kq % 2 == 1)

                def ev(e, ks=ks, pb=pb, n=n, c0=c0, on_act=on_act):
                    for q, k in enumerate(ks):
                        o = nT[:, k, c0:c0 + n]
                        i = self.psT[pb][:, q * 128:q * 128 + n]
                        if on_act:
                            ins = e.activation(out=o, in_=i, func=AF.Identity,
                                               bias=self.mcol[:, row, 0, k:k + 1],
                                               scale=self.gscol[:, row, k:k + 1])
                        else:
                            ins = e.tensor_scalar(out=o, in0=i, scalar1=self.gscol[:, row, k:k + 1],
                                                  scalar2=self.mcol[:, row, 0, k:k + 1],
                                                  op0=ALU.mult, op1=ALU.add)
                    return ins
                P.op("scalar" if on_act else "vector", ev,
                     reads=[self.b_psT[pb], self.b_const], writes=[nT_bufs[ti]])
            r0 += n
            c0 += n

    def norm_rows(self, pieces, row, nT, nT_buf):
        total = sum(n for _, _, n in pieces)
        s = self.ncount % 2
        hb, bh = self.hbuf[s], self.b_h[s]
        o = 0
        for ap, r0, n in pieces:
            self.P.dma("sync", (lambda hb=hb, ap=ap, r0=r0, n=n, o=o: lambda e: e.dma_start(
                out=hb[o:o + n, :], in_=ap[r0:r0 + n, :]))(), "h%d" % s, writes=[bh])
            o += n
        self.norm_tiles(None, 0, [total], row, nT, [nT_buf], preloaded=True)

    def alloc_wring(self, WT, nslots):
        self.WT = WT
        self.wslot = [self.sb([128, self.cfg.KC, WT], BF16, name="wslot") for _ in range(nslots)]
        self.b_w = [Buf("w%d" % i) for i in range(nslots)]
        for i in range(nslots):
            self.P.chan("w%d" % i)
        self.wsched = []
        self.wloaded = 0
        self.wused = 0

    def _wload_upto(self, idx):
        ns = len(self.wslot)
        while self.wloaded <= idx and self.wloaded < len(self.wsched):
            i = self.wloaded
            s = i % ns
            src = self.wsched[i].rearrange("(kc p) n -> p kc n", p=128)
            self.P.dma("gpsimd", (lambda s=s, src=src: lambda e: e.dma_start(out=self.wslot[s][:], in_=src))(),
                       "w%d" % s, writes=[self.b_w[s]])
            self.wloaded += 1

    def next_w(self):
        i = self.wused
        self.wused += 1
        ns = len(self.wslot)
        self._wload_upto(i + ns - 1)
        return self.wslot[i % ns], self.b_w[i % ns]

    def alloc_tm(self, stack):
        WT = self.WT
        H = WT // 128
        self.psU = [self.ps([128, WT], F32, stack, "psU") for _ in range(2)]
        self.b_psU = [Buf("psU0"), Buf("psU1")]
        self.psB = [self.ps([128, WT], BF16, stack, "psB") for _ in range(2)]
        self.b_psB = [Buf("psB0"), Buf("psB1")]
        self.ybuf = [self.sb([128, H, 128], F32, stack, "ybuf") for _ in range(2)]
        self.b_y = [Buf("y0"), Buf("y1")]
        self.sqb = self.sb([128, H, 128], F32, stack, "sqb")
        self.b_sq = Buf("sq")
        self.st4 = [self.sb([128, 3, H], F32, stack, "st4") for _ in range(2)]
        self.b_st4 = [Buf("st40"), Buf("st41")]
        self.yb = [self.sb([128, H, 128], BF16, stack, "yb") for _ in range(2)]
        self.b_yb = [Buf("yb0"), Buf("yb1")]
        self.rt = [self.sb([128, H, 2, 32], F32, stack, "rt") for _ in range(4)]
        self.b_rt = [Buf("rt%d" % i) for i in range(4)]
        self.tmc = 0
        self.tm_pending = None

    def tm_flush(self):
        if self.tm_pending is not None:
            f, self.tm_pending = self.tm_pending, None
            f()

    def tm_tile(self, wt, bw, nT, nT_buf, col0, mode, norm_bc, rope_tile, dest, dest_bufs):
        cfg, P = self.cfg, self.P
        KC, WT = cfg.KC, self.WT
        H = WT // 128
        i = self.tmc % 2
        self.tmc += 1
        psU, bpsU = self.psU[i], self.b_psU[i]

        def mm(e):
            for k in range(KC):
                ins = e.matmul(psU[:, 0:WT], lhsT=nT[:, k, col0:col0 + 128], rhs=wt[:, k, :],
                               start=(k == 0), stop=(k == KC - 1))
            return ins
        P.op("tensor", mm, reads=[nT_buf, bw], writes=[bpsU])
        self.tm_flush()
        if mode == "plain":
            P.op("scalar", lambda e: e.activation(out=dest, in_=psU[:, 0:WT], func=AF.Copy),
                 reads=[bpsU], writes=dest_bufs)
            return
        y, by = self.ybuf[i], self.b_y[i]
        yf = y[:].rearrange("p h d -> p (h d)")
        P.op("scalar", lambda e: e.activation(out=yf, in_=psU[:, 0:WT], func=AF.Copy), reads=[bpsU], writes=[by])
        if norm_bc is not None:
            s4, bs4 = self.st4[i], self.b_st4[i]
            P.op("vector", lambda e: e.tensor_tensor(out=self.sqb[:], in0=y[:], in1=y[:], op=ALU.mult),
                 reads=[by], writes=[self.b_sq])
            P.op("vector", lambda e: e.tensor_reduce(out=s4[:, 0, :], in_=self.sqb[:], axis=AX.X, op=ALU.add),
                 reads=[self.b_sq], writes=[bs4])
            P.op("vector", lambda e: e.tensor_scalar(out=s4[:, 1, :], in0=s4[:, 0, :], scalar1=1.0 / 128,
                                                     scalar2=EPS, op0=ALU.mult, op1=ALU.add),
                 reads=[bs4], writes=[bs4])
            P.op("gpsimd", lambda e: e.tensor_tensor(out=s4[:, 2, :], in0=s4[:, 1, :], in1=self.mhalf[:, 0:H],
                                                     op=ALU.pow), reads=[bs4, self.b_const], writes=[bs4])
            P.op("vector", lambda e: e.tensor_tensor(
                out=y[:], in0=y[:], in1=s4[:, 2, :].unsqueeze(2).broadcast_to([128, H, 128]), op=ALU.mult),
                reads=[by, bs4], writes=[by])
            P.op("vector", lambda e: e.tensor_tensor(
                out=y[:], in0=y[:], in1=_bc(norm_bc, [[0, H], [1, 128]]), op=ALU.mult),
                reads=[by, self.b_const], writes=[by])
        yb, byb = self.yb[i], self.b_yb[i]
        if rope_tile is not None:
            v5 = y[:].rearrange("p h (a b i) -> p h a b i", a=2, b=2)
            o5 = yb[:].rearrange("p h (a b i) -> p h a b i", a=2, b=2)
            x1, x2 = v5[:, :, :, 0, :], v5[:, :, :, 1, :]
            cosap = _bc(self.cs[:, rope_tile, :], [[0, H], [32, 2], [1, 32]])
            sinap = _bc(self.sn[:, rope_tile, :], [[0, H], [32, 2], [1, 32]])
            r = self.rt
            br = self.b_rt
            P.op("vector", lambda e: e.tensor_tensor(out=r[0][:], in0=x1, in1=cosap, op=ALU.mult),
                 reads=[by, self.b_const], writes=[br[0]])
            P.op("vector", lambda e: e.tensor_tensor(out=r[1][:], in0=x2, in1=sinap, op=ALU.mult),
                 reads=[by, self.b_const], writes=[br[1]])
            P.op("vector", lambda e: e.tensor_tensor(out=r[2][:], in0=x2, in1=cosap, op=ALU.mult),
                 reads=[by, self.b_const], writes=[br[2]])
            P.op("vector", lambda e: e.tensor_tensor(out=r[3][:], in0=x1, in1=sinap, op=ALU.mult),
                 reads=[by, self.b_const], writes=[br[3]])
            P.op("vector", lambda e: e.tensor_tensor(out=o5[:, :, :, 0, :], in0=r[0][:], in1=r[1][:],
                                                     op=ALU.subtract), reads=[br[0], br[1]], writes=[byb])
            P.op("vector", lambda e: e.tensor_tensor(out=o5[:, :, :, 1, :], in0=r[2][:], in1=r[3][:],
                                                     op=ALU.add), reads=[br[2], br[3]], writes=[byb])
        else:
            P.op("vector", lambda e: e.tensor_copy(out=yb[:], in_=y[:]), reads=[by], writes=[byb])
        psB, bpsB = self.psB[i], self.b_psB[i]

        def part2():
            def tr(e):
                for hh in range(H):
                    ins = e.transpose(out=psB[:, hh * 128:(hh + 1) * 128], in_=yb[:, hh, :], identity=self.identb[:])
                return ins
            P.op("tensor", tr, reads=[byb, self.b_const], writes=[bpsB])
            P.op("scalar", lambda e: e.activation(out=dest, in_=psB[:, 0:WT].rearrange("p (h t) -> p h t", h=H),
                                                  func=AF.Copy), reads=[bpsB], writes=dest_bufs)
        self.tm_pending = part2


def token_groups(cfg, GT, with_ctx):
    gs = []
    for g0 in range(0, cfg.TOK, GT):
        n = min(GT, cfg.TOK - g0)
        gs.append((0, g0, [128] * (n // 128)))
    if with_ctx:
        assert cfg.CTX <= GT
        gs.append((1, 0, [128] * (cfg.CTX // 128)))
    return gs


def build_kv(cfg, GT=512, F=None):
    from contextlib import ExitStack
    nc = F.nc if F else bass.Bass("TRN2", target_bir_lowering=False)
    D, KC, TOK, CTX = cfg.D, cfg.KC, cfg.TOK, cfg.CTX
    TA = TOK + CTX
    WT = cfg.WT
    H = WT // 128
    NKA, NKC = cfg.A_KV, 2 * cfg.C_H
    dram = dict(F.dram) if F else {}

    def din(name, shape, dt=F32):
        dram[name] = nc.dram_tensor(name, list(shape), dt, kind="ExternalInput").ap()

    def dout(name, shape, dt):
        dram[name] = nc.dram_tensor(name, list(shape), dt, kind="ExternalOutput").ap()

    if not F:
        din("h", [TOK, D]); din("hc", [CTX, D]); din("ident", [128, 128])
        din("mcol", [128, 2, 3, KC]); din("ng_col", [128, KC])
        din("cs", [TOK, 64]); din("sn", [TOK, 64]); din("kn", [1, 128])
        din("wkv", [D, cfg.KV])
        dout("ktA", [NKA, 128, TA], BF16); dout("vA", [TA, cfg.VA], BF16)
        dout("ktC", [NKC, 128, TA], BF16); dout("vC", [TA, cfg.VC], BF16)

    with ExitStack() as st:
        P = F.P if F else Prog(nc, st)
        L = Layer(cfg, nc, st, P, GT)
        knbc = L.sb([128, 128], F32, name="knbc")

        def extra(dr, bc):
            P.dma("sync", lambda e: e.dma_start(out=knbc[:], in_=dr["kn"].partition_broadcast(128)), "const",
                  writes=[bc])
        L.extra_const_loads = extra
        L.load_consts(dram)
        L.alloc_wring(WT, 2)
        L.alloc_norm(st)
        L.alloc_tm(st)
        nT = L.sb([128, KC, GT], BF16, name="nT")
        ktA = L.sb([128, NKA, GT], BF16, name="ktA")
        ktC = L.sb([128, NKC, GT], BF16, name="ktC")
        b_ktA, b_ktC = Buf("ktA"), Buf("ktC")
        vst = [L.sb([128, WT], BF16, name="vst") for _ in range(2)]
        b_vst = [Buf("vst0"), Buf("vst1")]
        P.chan("vst0"); P.chan("vst1"); P.chan("koutA"); P.chan("koutC")
        if F:
            groups = F.kv_groups
        else:
            groups = [(k, dram["h"] if k == 0 else dram["hc"], g0, tl, (g0 if k == 0 else TOK), g0)
                      for k, g0, tl in token_groups(cfg, GT, True)]
        segs = []
        for c0 in range(0, cfg.KV, WT):
            if c0 < cfg.KA:
                segs.append(("ka", c0))
            elif c0 < cfg.KA + cfg.VA:
                segs.append(("va", c0))
            elif c0 < cfg.KA + cfg.VA + cfg.KCC:
                segs.append(("kc", c0))
            else:
                segs.append(("vc", c0))
        for _ in groups:
            for _, c0 in segs:
                L.wsched.append(dram["wkv"][:, c0:c0 + WT])
        vcount = 0
        nT2 = L.sb([128, KC, GT], BF16, name="nT2")
        nTs = [nT, nT2]
        nbufs_all = [[Buf("nT%d_%d" % (gi, i)) for i in range(len(g[3]))] for gi, g in enumerate(groups)]
        k0_, s0_, r0_, t0_ = groups[0][0], groups[0][1], groups[0][2], groups[0][3]
        L.norm_tiles(s0_, r0_, t0_, k0_, nTs[0], nbufs_all[0])
        for gi, (kind, src, g0, tiles, tok0, rope_row0) in enumerate(groups):
            nT = nTs[gi % 2]
            nbufs = nbufs_all[gi]
            GTg = 128 * len(tiles)
            nxt = groups[gi + 1] if gi + 1 < len(groups) else None
            nxt_done = 0
            if kind == 0:
                L.load_rope(rope_row0, len(tiles))
            for wi, (name, c0) in enumerate(segs):
                wt, bw = L.next_w()
                for ti in range(len(tiles)):
                    t0 = tok0 + ti * 128
                    rope_tile = ti if kind == 0 else None
                    if name == "ka":
                        h0 = c0 // 128
                        L.tm_tile(wt, bw, nT, nbufs[ti], ti * 128, "heads", knbc[:], rope_tile,
                                  ktA[:, h0:h0 + H, ti * 128:(ti + 1) * 128], [b_ktA])
                    elif name == "kc":
                        h0 = (c0 - cfg.KA - cfg.VA) // 128
                        L.tm_tile(wt, bw, nT, nbufs[ti], ti * 128, "heads", None, rope_tile,
                                  ktC[:, h0:h0 + H, ti * 128:(ti + 1) * 128], [b_ktC])
                    else:
                        s = vcount % 2
                        vcount += 1
                        L.tm_tile(wt, bw, nT, nbufs[ti], ti * 128, "plain", None, None, vst[s][:, :], [b_vst[s]])
                        if name == "va":
                            dst = dram["vA"][t0:t0 + 128, c0 - cfg.KA:c0 - cfg.KA + WT]
                        else:
                            cc = c0 - cfg.KA - cfg.VA - cfg.KCC
                            dst = dram["vC"][t0:t0 + 128, cc:cc + WT]
                        P.dma("sync", (lambda dst=dst, s=s: lambda e: e.dma_start(out=dst, in_=vst[s][:, :]))(),
                              "vst%d" % s, reads=[b_vst[s]])
                if nxt is not None:
                    upto = len(nxt[3]) if wi == len(segs) - 1 else min(len(nxt[3]), wi + 1)
                    while nxt_done < upto:
                        L.norm_tiles(nxt[1], nxt[2] + 128 * nxt_done, [nxt[3][nxt_done]], nxt[0],
                                     nTs[(gi + 1) % 2], [nbufs_all[gi + 1][nxt_done]], col0=128 * nxt_done)
                        nxt_done += 1
            L.tm_flush()
            P.dma("sync", (lambda tok0=tok0, GTg=GTg: lambda e: e.dma_start(
                out=dram["ktA"][:, :, tok0:tok0 + GTg].rearrange("h d t -> d h t"), in_=ktA[:, :, 0:GTg]))(),
                "koutA", reads=[b_ktA])
            P.dma("sync", (lambda tok0=tok0, GTg=GTg: lambda e: e.dma_start(
                out=dram["ktC"][:, :, tok0:tok0 + GTg].rearrange("h d t -> d h t"), in_=ktC[:, :, 0:GTg]))(),
                "koutC", reads=[b_ktC])
        P.emit()
    return nc


def rope_tables(cfg):
    n = cfg.SEQ
    gw = 64
    rows = n // gw
    row = np.broadcast_to(np.arange(rows)[:, None], (rows, gw)).reshape(-1).astype(np.float32)
    col = np.broadcast_to(np.arange(gw)[None, :], (rows, gw)).reshape(-1).astype(np.float32)
    inv = (np.float32(10000.0) ** (-np.arange(0, 64, 2, dtype=np.float32) / np.float32(64))).astype(np.float32)
    ar = (row[:, None] * inv).astype(np.float32)
    ac = (col[:, None] * inv).astype(np.float32)
    cs = np.concatenate([np.cos(ar), np.cos(ac)], axis=1).astype(np.float32)
    sn = np.concatenate([np.sin(ar), np.sin(ac)], axis=1).astype(np.float32)
    return cs, sn


def col_layout(v, kc):
    v = np.asarray(v, np.float32)
    lead = v.shape[:-1]
    a = v.reshape(lead + (kc, 128))
    return np.ascontiguousarray(np.moveaxis(a, -1, 0))


def common_inputs(cfg, l, b, j, hs, hcs, m, norm_g, cs, sn):
    mb = m[b][l]
    mcol = col_layout(mb.reshape(2, 3, cfg.D), cfg.KC)
    sl = slice(j * cfg.TOK, (j + 1) * cfg.TOK)
    return {
        "h": np.ascontiguousarray(hs[b][sl]),
        "hc": np.ascontiguousarray(hcs[b]),
        "ident": np.eye(128, dtype=np.float32),
        "mcol": mcol,
        "ng_col": col_layout(norm_g[l], cfg.KC),
        "cs": np.ascontiguousarray(cs[sl]),
        "sn": np.ascontiguousarray(sn[sl]),
    }


def run_kv(cfg, l, hs, hcs, m, norm_g, k_norm_a, w_in, cs, sn, GT=512):
    nc = build_kv(cfg, GT)
    wkv = np.ascontiguousarray(w_in[l][:, :cfg.KV])
    in_maps = []
    for r in range(N_CORES):
        b, j = divmod(r, RANKS)
        d = common_inputs(cfg, l, b, j, hs, hcs, m, norm_g, cs, sn)
        d["kn"] = np.ascontiguousarray(k_norm_a[l].reshape(1, 128)).astype(np.float32)
        d["wkv"] = wkv
        in_maps.append(d)
    res = run_bass_kernel_spmd(nc, in_maps, core_ids=list(range(N_CORES))).results
    out = []
    T = cfg.TOK
    for b in range(cfg.BATCH):
        rs = [res[b * RANKS + j] for j in range(RANKS)]
        kv = {}
        for nm in ("ktA", "ktC"):
            kv[nm] = np.ascontiguousarray(np.concatenate([rs[0][nm][:, :, T:]] + [x[nm][:, :, :T] for x in rs], axis=2))
        for nm in ("vA", "vC"):
            kv[nm] = np.ascontiguousarray(np.concatenate([rs[0][nm][T:]] + [x[nm][:T] for x in rs], axis=0))
        out.append(kv)
    return out


DEBUG = False


def build_rest(cfg, l, last, GT=512, F=None):
    from contextlib import ExitStack
    nc = F.nc if F else bass.Bass("TRN2", target_bir_lowering=False)
    D, KC, TOK, CTX = cfg.D, cfg.KC, cfg.TOK, cfg.CTX
    SK = CTX + cfg.SEQ
    WT = cfg.WT
    HB = WT // 128
    A_H, A_KV, C_H = cfg.A_H, cfg.A_KV, cfg.C_H
    NBC = cfg.B_W // 128
    NCHUNK = D // 128
    NG = (TOK + GT - 1) // GT
    HN = 2 * RANKS + 2 * (NG - 1)
    with_ctx = not last
    lambda_init = 0.8 - 0.6 * math.exp(-0.3 * l)
    QW = cfg.IN - cfg.KV
    o_qa, o_qc = 0, cfg.A_W
    o_xb = o_qc + 2 * 128 * C_H
    o_bb = o_xb + cfg.B_W
    o_cb = o_bb + cfg.B_W
    o_za = o_cb + cfg.B_W
    o_zb = o_za + cfg.A_W
    o_zc = o_zb + cfg.B_W
    assert o_zc + cfg.C_W == QW
    dram = dict(F.dram) if F else {}
    NSEL = F.nsel if F else NG * 2 * HN

    def din(name, shape, dt=F32):
        dram[name] = nc.dram_tensor(name, list(shape), dt, kind="ExternalInput").ap()

    def dout(name, shape, dt):
        dram[name] = nc.dram_tensor(name, list(shape), dt, kind="ExternalOutput").ap()

    if not F:
        din("h", [TOK, D]); din("hc", [CTX, D]); din("ident", [128, 128])
        din("mcol", [128, 2, 3, KC]); din("ng_col", [128, KC])
        din("cs", [TOK, 64]); din("sn", [TOK, 64])
        din("hh", [HN, D]); din("sel", [1, NG * 2 * HN]); din("gate_row", [2, D])
        din("qn", [1, 128]); din("convw", [128, NBC, 3]); din("lam4", [4, 128]); din("sg_col", [128, 2])
        din("wq", [D, QW]); din("wo", [D, D])
        din("ktA", [A_KV, 128, SK], BF16); din("vA", [SK, cfg.VA], BF16)
        din("ktC", [2 * C_H, 128, SK], BF16); din("vC", [SK, cfg.VC], BF16)
        if last:
            din("fg_row", [1, D])
            dram["h2"] = nc.dram_tensor("h2", [TOK, D], F32).ap()
        dout("ho", [TOK, D], F32)
        if with_ctx:
            dout("hco", [CTX, D], F32)
        if DEBUG:
            dout("dbgA", [128, NCHUNK, GT], BF16)
            dout("dbgB", [128, NCHUNK, GT], BF16)
            dout("dbgG", [128, A_H + 2 * C_H, GT], BF16)
    if last:
        h2 = dram["h2"]

    with ExitStack() as st:
        P = F.P if F else Prog(nc, st)
        L = Layer(cfg, nc, st, P, GT)
        qnbc = L.sb([128, 128], F32, name="qnbc")
        convcol = L.sb([128, NBC, 3], F32, name="convcol")
        lamb = L.sb([128, 4, 128], F32, name="lamb")
        lamt = L.sb([128, 2, 128], F32, name="lamt")
        lams = L.sb([128, 8], F32, name="lams")
        sgcol = L.sb([128, 2], F32, name="sgcol")
        selbc = L.sb([128, NSEL], F32, name="selbc")
        onesb = L.sb([128, 128], BF16, name="onesb")
        ones32 = L.sb([128, 128], F32, name="ones32")
        mhalfw = L.sb([128, GT], F32, name="mhalfw")

        def extra(dr, bc):
            def ld(o, i):
                P.dma("sync", lambda e: e.dma_start(out=o, in_=i), "const", writes=[bc])
            ld(qnbc[:], dr["qn"].partition_broadcast(128))
            ld(convcol[:], dr["convw"])
            for i in range(4):
                ld(lamb[:, i, :], dr["lam4"][i:i + 1, :].partition_broadcast(128))
            ld(sgcol[:], dr["sg_col"])
            ld(selbc[:], dr["sel"].partition_broadcast(128))
        L.extra_const_loads = extra
        L.load_consts(dram)
        bc = L.b_const
        P.op("gpsimd", lambda e: e.memset(onesb[:], 1.0), writes=[bc])
        P.op("gpsimd", lambda e: e.memset(ones32[:], 1.0), writes=[bc])
        P.op("gpsimd", lambda e: e.memset(mhalfw[:], -0.5), writes=[bc])
        P.op("vector", lambda e: e.tensor_tensor(
            out=lamt[:], in0=lamb[:].rearrange("p (a b) d -> p a b d", b=2)[:, :, 0, :],
            in1=lamb[:].rearrange("p (a b) d -> p a b d", b=2)[:, :, 1, :], op=ALU.mult), reads=[bc], writes=[bc])
        P.op("vector", lambda e: e.tensor_reduce(out=lams[:, 0:2], in_=lamt[:], axis=AX.X, op=ALU.add),
             reads=[bc], writes=[bc])
        P.op("scalar", lambda e: e.activation(out=lams[:, 2:4], in_=lams[:, 0:2], func=AF.Exp), reads=[bc], writes=[bc])
        P.op("vector", lambda e: e.tensor_tensor(out=lams[:, 4:5], in0=lams[:, 3:4], in1=lams[:, 2:3],
                                                 op=ALU.subtract), reads=[bc], writes=[bc])
        P.op("vector", lambda e: e.tensor_scalar(out=lams[:, 5:6], in0=lams[:, 4:5], scalar1=-lambda_init,
                                                 scalar2=None, op0=ALU.add), reads=[bc], writes=[bc])
        P.op("vector", lambda e: e.tensor_scalar(out=sgcol[:], in0=sgcol[:], scalar1=1.0 - lambda_init,
                                                 scalar2=None, op0=ALU.mult), reads=[bc], writes=[bc])
        neglam = lams[:, 5:6]

        L.alloc_wring(WT, 2)
        mixT = L.sb([128, NCHUNK, GT], BF16, name="mixT")
        b_mix = [Buf("mix%d" % i) for i in range(NCHUNK)]
        NGATE = A_H + 2 * C_H
        gateT = L.sb([128, NGATE, GT], BF16, name="gateT")
        b_gate = [Buf("gate%d" % i) for i in range(NGATE)]
        nTh = L.sb([128, KC, HN], BF16, name="nTh")
        b_nTh = Buf("nTh")
        if last:
            ssq = L.sb([128, TOK // 128, D // WT], F32, name="ssq")
            b_ssq = Buf("ssq")
        if F:
            groups = F.rest_groups
        else:
            groups = []
            for k, g0, tl in token_groups(cfg, GT, with_ctx):
                if k == 0:
                    groups.append((0, dram["h"], g0, tl, (h2 if last else dram["ho"]), g0, g0,
                                   (g0 // GT) * 2 * HN, g0 // 128,
                                   ([(dram["hh"], 0, HN)] if g0 == 0 else None)))
                else:
                    groups.append((1, dram["hc"], 0, tl, dram["hco"], 0, None, None, None, None))

        wq, wo = dram["wq"], dram["wo"]
        per_group = []
        for c0 in range(o_qa, o_qa + cfg.A_W, WT):
            per_group.append(("qa", c0))
        for c0 in range(o_qc, o_qc + 2 * 128 * C_H, WT):
            per_group.append(("qc", c0))
        for mi in range(cfg.B_W // WT):
            for nm, o in (("xb", o_xb), ("cb", o_cb), ("bb", o_bb), ("zb", o_zb)):
                per_group.append((nm, o + mi * WT))
        for c0 in range(o_za, o_za + cfg.A_W, WT):
            per_group.append(("za", c0))
        for c0 in range(o_zc, o_zc + cfg.C_W, WT):
            per_group.append(("zc", c0))
        for _ in groups:
            for nm, c0 in per_group:
                L.wsched.append(wq[:, c0:c0 + WT])
            for c0 in range(0, D, WT):
                L.wsched.append(wo[:, c0:c0 + WT])

        for gi_, (kind, src, g0, tiles, dst, d0, rope_row0, sel_off, ssq_t0, halo_src) in enumerate(groups):
            GTg = 128 * len(tiles)
            if kind == 0:
                L.load_rope(rope_row0, len(tiles))
            with ExitStack() as sA:
                nT = L.sb([128, KC, GT], BF16, sA, "nT")
                nbufs = [Buf("nT%d" % i) for i in range(len(tiles))]
                with ExitStack() as s1:
                    L.alloc_norm(s1)
                    if halo_src is not None:
                        L.norm_rows(halo_src, 0, nTh, b_nTh)
                    L.norm_tiles(src, g0, tiles, kind, nT, nbufs)
                    P.emit("A1")
                with ExitStack() as s2:
                    L.alloc_tm(s2)
                    psF = [L.ps([128, GT], F32, s2, "psF") for _ in range(2)]
                    b_psF = [Buf("psF0"), Buf("psF1")]
                    psH = L.ps([128, 512], F32, s2, "psH")
                    b_psH = Buf("psH")
                    bufX = L.sb([128, HB, GT + 2], F32, s2, "bufX")
                    bufXh = L.sb([128, HB, HN], F32, s2, "bufXh")
                    bufC = L.sb([128, HB, GT], F32, s2, "bufC")
                    bufS = [L.sb([128, GT], F32, s2, "bufS") for _ in range(2)]
                    tmpH = L.sb([128, 2, HN], F32, s2, "tmpH")
                    b_X = [Buf("X%d" % i) for i in range(HB)]
                    b_Xh = [Buf("Xh%d" % i) for i in range(HB)]
                    b_C = [Buf("C%d" % i) for i in range(HB)]
                    b_S = [Buf("S0"), Buf("S1")]
                    b_tH = Buf("tH")
                    fcount = [0]

                    def fm_mm(wt, bw, jb, halo):
                        i = fcount[0] % 2
                        fcount[0] += 1
                        pf, bpf = psF[i], b_psF[i]

                        def mm(e):
                            for k in range(KC):
                                ins = e.matmul(pf[:, 0:GTg], lhsT=wt[:, k, jb * 128:(jb + 1) * 128],
                                               rhs=nT[:, k, 0:GTg], start=(k == 0), stop=(k == KC - 1))
                            return ins
                        P.op("tensor", mm, reads=nbufs + [bw], writes=[bpf])
                        if halo:
                            def mmh(e):
                                for k in range(KC):
                                    ins = e.matmul(psH[:, 0:HN], lhsT=wt[:, k, jb * 128:(jb + 1) * 128],
                                                   rhs=nTh[:, k, :], start=(k == 0), stop=(k == KC - 1))
                                return ins
                            P.op("tensor", mmh, reads=[b_nTh, bw], writes=[b_psH])
                        return pf, bpf

                    scount = 0
                    for nm, c0 in per_group:
                        wt, bw = L.next_w()
                        if nm in ("qa", "qc"):
                            for ti in range(len(tiles)):
                                rope_tile = ti if kind == 0 else None
                                if nm == "qa":
                                    ch0 = (c0 - o_qa) // 128
                                    nb = qnbc[:]
                                else:
                                    ch0 = A_H + NBC + (c0 - o_qc) // 128
                                    nb = None
                                L.tm_tile(wt, bw, nT, nbufs[ti], ti * 128, "heads", nb, rope_tile,
                                          mixT[:, ch0:ch0 + HB, ti * 128:(ti + 1) * 128],
                                          [b_mix[ch0 + q] for q in range(HB)])
                        elif nm in ("za", "zc"):
                            L.tm_flush()
                            g_0 = (c0 - o_za) // 128 if nm == "za" else A_H + (c0 - o_zc) // 128
                            for jb in range(HB):
                                pf, bpf = fm_mm(wt, bw, jb, False)
                                P.op("scalar", (lambda pf=pf, gi=g_0 + jb: lambda e: e.activation(
                                    out=gateT[:, gi, 0:GTg], in_=pf[:, 0:GTg], func=AF.Silu))(),
                                    reads=[bpf], writes=[b_gate[g_0 + jb]])
                        else:
                            L.tm_flush()
                            halo = (kind == 0)
                            blk0 = ((c0 - {"xb": o_xb, "cb": o_cb, "bb": o_bb, "zb": o_zb}[nm]) // 128)
                            for jb in range(HB):
                                cblk = blk0 + jb
                                pf, bpf = fm_mm(wt, bw, jb, halo and nm in ("xb", "cb"))
                                X = bufX[:, jb, 1:GTg + 1]
                                if nm == "xb":
                                    P.op("scalar", (lambda pf=pf, X=X: lambda e: e.activation(
                                        out=X, in_=pf[:, 0:GTg], func=AF.Copy))(), reads=[bpf], writes=[b_X[jb]])
                                    if halo:
                                        P.op("scalar", (lambda jb=jb: lambda e: e.activation(
                                            out=bufXh[:, jb, :], in_=psH[:, 0:HN], func=AF.Copy))(),
                                            reads=[b_psH], writes=[b_Xh[jb]])
                                elif nm == "cb":
                                    P.op("vector", (lambda pf=pf, X=X: lambda e: e.tensor_tensor(
                                        out=X, in0=pf[:, 0:GTg], in1=X, op=ALU.mult))(),
                                        reads=[bpf, b_X[jb]], writes=[b_X[jb]])
                                    pads = _bc(bufX[:, jb, 0:1], [[GTg + 1, 2]])
                                    if halo:
                                        P.op("vector", (lambda jb=jb: lambda e: e.tensor_tensor(
                                            out=bufXh[:, jb, :], in0=psH[:, 0:HN], in1=bufXh[:, jb, :],
                                            op=ALU.mult))(), reads=[b_psH, b_Xh[jb]], writes=[b_Xh[jb]])
                                        selap = selbc[:, sel_off:sel_off + 2 * HN].rearrange(
                                            "p (a h) -> p a h", a=2)
                                        P.op("vector", (lambda jb=jb, selap=selap: lambda e: e.tensor_tensor(
                                            out=tmpH[:], in0=_bc(bufXh[:, jb, :], [[0, 2], [1, HN]]), in1=selap,
                                            op=ALU.mult))(), reads=[b_Xh[jb], bc], writes=[b_tH])
                                        P.op("vector", (lambda pads=pads: lambda e: e.tensor_reduce(
                                            out=pads, in_=tmpH[:], axis=AX.X, op=ALU.add))(),
                                            reads=[b_tH], writes=[b_X[jb]])
                                    else:
                                        P.op("vector", (lambda pads=pads: lambda e: e.memset(pads, 0.0))(),
                                             writes=[b_X[jb]])
                                    Cb = bufC[:, jb, 0:GTg]
                                    P.op("vector", (lambda jb=jb, Cb=Cb, cblk=cblk: lambda e: e.tensor_scalar(
                                        out=Cb, in0=bufX[:, jb, 0:GTg], scalar1=convcol[:, cblk, 0:1], scalar2=None,
                                        op0=ALU.mult))(), reads=[b_X[jb], bc], writes=[b_C[jb]])
                                    for tap in (1, 2):
                                        P.op("vector", (lambda jb=jb, Cb=Cb, cblk=cblk, tap=tap: lambda e:
                                                        e.scalar_tensor_tensor(
                                                            out=Cb, in0=bufX[:, jb, tap:tap + GTg],
                                                            scalar=convcol[:, cblk, tap:tap + 1], in1=Cb,
                                                            op0=ALU.mult, op1=ALU.add))(),
                                             reads=[b_X[jb], b_C[jb], bc], writes=[b_C[jb]])
                                elif nm == "bb":
                                    Cb = bufC[:, jb, 0:GTg]
                                    P.op("vector", (lambda pf=pf, Cb=Cb: lambda e: e.tensor_tensor(
                                        out=Cb, in0=pf[:, 0:GTg], in1=Cb, op=ALU.mult))(),
                                        reads=[bpf, b_C[jb]], writes=[b_C[jb]])
                                else:
                                    si = scount % 2
                                    scount += 1
                                    P.op("scalar", (lambda pf=pf, si=si: lambda e: e.activation(
                                        out=bufS[si][:, 0:GTg], in_=pf[:, 0:GTg], func=AF.Silu))(),
                                        reads=[bpf], writes=[b_S[si]])
                                    P.op("vector", (lambda jb=jb, si=si, cblk=cblk: lambda e: e.tensor_tensor(
                                        out=mixT[:, A_H + cblk, 0:GTg], in0=bufC[:, jb, 0:GTg],
                                        in1=bufS[si][:, 0:GTg], op=ALU.mult))(),
                                        reads=[b_C[jb], b_S[si]], writes=[b_mix[A_H + cblk]])
                    if DEBUG and gi_ == 0:
                        P.chan("dbg")
                        P.dma("sync", lambda e: e.dma_start(out=dram["dbgA"][:, :, :], in_=mixT[:]), "dbg", reads=b_mix)
                        P.dma("sync", lambda e: e.dma_start(out=dram["dbgG"][:, :, :], in_=gateT[:]), "dbg", reads=b_gate)
                    P.emit("A2")

            SKg = SK if kind == 0 else CTX
            NKB = SKg // 128
            with ExitStack() as sB:
                ktb = [L.sb([128, SK], BF16, sB, "ktb") for _ in range(2)]
                NVB = 4
                vtb = [L.sb([128, SK // 128, 128], BF16, sB, "vtb") for _ in range(NVB)]
                b_kt = [Buf("kt0"), Buf("kt1")]
                b_vt = [Buf("vt%d" % i) for i in range(NVB)]
                for nm_ in ["kt0", "kt1"] + ["vt%d" % i for i in range(NVB)]:
                    P.chan(nm_)
                NPB, NSS, LA = 3, 2, 1
                pbuf = [L.sb([128, 2, GT], BF16, sB, "pbuf") for _ in range(NPB)]
                b_p = [Buf("p%d" % i) for i in range(NPB)]
                Layer.UID += 1
                psS = [sB.enter_context(nc.psum_tensor("psS2_%d_%d" % (Layer.UID, i), [128, 1024], F32))
                       for i in range(NSS)]
                b_psS = [Buf("psS%d" % i) for i in range(NSS)]
                psO = [L.ps([128, GT], F32, sB, "psO") for _ in range(2)]
                b_psO = [Buf("psO0"), Buf("psO1")]
                psSums = [L.ps([128, GT], F32, sB, "psSum") for _ in range(2)]
                b_psSums = [Buf("psSum0"), Buf("psSum1")]
                rsb = L.sb([128, GT], F32, sB, "rsb")
                b_rsb = Buf("rsb")
                obuf = [L.sb([128, GT], F32, sB, "obuf") for _ in range(2)]
                b_ob = [Buf("ob0"), Buf("ob1")]
                ocn = [[L.sb([128, GT], F32, sB, "ocn") for _ in range(2)] for _ in range(2)]
                b_ocn = [[Buf("ocn%d%d" % (c, hf)) for hf in range(2)] for c in range(2)]
                dbuf = [L.sb([128, GT], F32, sB, "dbuf") for _ in range(2)]
                b_d = [Buf("d0"), Buf("d1")]
                sqd, b_sqd = ocn[0], b_ocn[0]
                rbuf, b_rb = ocn[1], b_ocn[1]
                cnt = {"kv": 0, "kb": 0, "ob": 0, "unit": 0}

                def attn_unit(kt, bkt, vts, qch):
                    qap = mixT[:, qch, 0:GTg]
                    u = cnt["unit"]
                    cnt["unit"] += 1
                    if len(vts) == 1:
                        outs = [(psO[u % 2], b_psO[u % 2])]
                    else:
                        outs = [(psO[0], b_psO[0]), (psO[1], b_psO[1])]
                    psSum, b_psSum = psSums[u % 2], b_psSums[u % 2]

                    steps = [(k0, min(2, NKB - k0)) for k0 in range(0, NKB, 2)]

                    def S(si, slot):
                        k0, nb = steps[si]

                        def f(e):
                            for b_ in range(nb):
                                ins = e.matmul(psS[slot][:, b_ * 512:b_ * 512 + GTg],
                                               lhsT=kt[:, (k0 + b_) * 128:(k0 + b_ + 1) * 128], rhs=qap,
                                               start=True, stop=True)
                            return ins
                        P.op("tensor", f, reads=[bkt, b_mix[qch]], writes=[b_psS[slot]])
                    base = cnt["kb"]
                    for si in range(min(LA, len(steps))):
                        S(si, (base + si) % NSS)
                    for si, (k0, nb) in enumerate(steps):
                        n_ = base + si
                        if si + LA < len(steps):
                            S(si + LA, (n_ + LA) % NSS)
                        ps_, pp = n_ % NSS, n_ % NPB
                        P.op("scalar", (lambda ps_=ps_, pp=pp, nb=nb: lambda e: e.activation(
                            out=pbuf[pp][:, 0:nb, 0:GTg],
                            in_=psS[ps_][:].rearrange("p (b n) -> p b n", b=2)[:, 0:nb, 0:GTg],
                            func=AF.Exp, scale=ATTN_SCALE))(),
                            reads=[b_psS[ps_]], writes=[b_p[pp]])

                        def pv(e, k0=k0, nb=nb, pp=pp):
                            for b_ in range(nb):
                                kb = k0 + b_
                                for (po, _), (vt, _) in zip(outs, vts):
                                    e.matmul(po[:, 0:GTg], lhsT=vt[:, kb, :], rhs=pbuf[pp][:, b_, 0:GTg],
                                             start=(kb == 0), stop=(kb == NKB - 1))
                                ins = e.matmul(psSum[:, 0:GTg], lhsT=onesb[:], rhs=pbuf[pp][:, b_, 0:GTg],
                                               start=(kb == 0), stop=(kb == NKB - 1))
                            return ins
                        P.op("tensor", pv, reads=[b_p[pp], bc] + [bv for _, bv in vts],
                             writes=[bo for _, bo in outs] + [b_psSum])
                    NKBs = len(steps)
                    cnt["kb"] = base + NKBs
                    P.op("vector", lambda e: e.reciprocal(out=rsb[:, 0:GTg], in_=psSum[:, 0:GTg]),
                         reads=[b_psSum], writes=[b_rsb])
                    return outs

                def load_k(ktsrc, s_):
                    P.dma("sync", lambda e: e.dma_start(out=ktb[s_][:, 0:SKg], in_=ktsrc[:, 0:SKg]), "kt%d" % s_,
                          writes=[b_kt[s_]])

                def load_v(vsrc, s_):
                    P.dma("sync", lambda e: e.dma_start(
                        out=vtb[s_][:, 0:NKB, :],
                        in_=vsrc.rearrange("(kb p) e -> p kb e", p=128)[:, 0:NKB, :]), "vt%d" % s_,
                        writes=[b_vt[s_]])

                segs_b = []
                vn = 0
                for g in range(A_KV):
                    segs_b.append({"k": dram["ktA"][g], "v": [(dram["vA"][:, g * 128:(g + 1) * 128], vn % NVB)],
                                   "vb": [vn % NVB], "kind": "A", "g": g})
                    vn += 1
                for hc in range(C_H):
                    vb = [vn % NVB, (vn + 1) % NVB]
                    vn += 2
                    for c in range(2):
                        segs_b.append({"k": dram["ktC"][2 * hc + c],
                                       "v": ([(dram["vC"][:, hc * 256 + hf * 128:hc * 256 + (hf + 1) * 128], vb[hf])
                                              for hf in range(2)] if c == 0 else []),
                                       "vb": vb, "kind": "C", "hc": hc, "c": c})

                def seg_load(si):
                    sg = segs_b[si]
                    load_k(sg["k"], si % 2)
                    for vsrc, bi in sg["v"]:
                        load_v(vsrc, bi)
                seg_load(0)


                for g in range(A_KV):
                    s_ = g % 2
                    sv = segs_b[g]["vb"][0]
                    if g + 1 < len(segs_b):
                        seg_load(g + 1)
                    for q in range(4):
                        hq = g * 4 + q
                        (po, bpo), = attn_unit(ktb[s_], b_kt[s_], [(vtb[sv], b_vt[sv])], hq)
                        oi = cnt["ob"] % 2
                        cnt["ob"] += 1
                        P.op("vector", (lambda oi=oi, po=po: lambda e: e.tensor_tensor(
                            out=obuf[oi][:, 0:GTg], in0=po[:, 0:GTg], in1=rsb[:, 0:GTg], op=ALU.mult))(),
                            reads=[bpo, b_rsb], writes=[b_ob[oi]])
                        P.op("vector", (lambda oi=oi, hq=hq: lambda e: e.tensor_tensor(
                            out=mixT[:, hq, 0:GTg], in0=obuf[oi][:, 0:GTg], in1=gateT[:, hq, 0:GTg],
                            op=ALU.mult))(), reads=[b_ob[oi], b_gate[hq]], writes=[b_mix[hq]])
                for hc in range(C_H):
                    ch0 = A_H + NBC + 2 * hc
                    for c in range(2):
                        si = A_KV + 2 * hc + c
                        s_ = si % 2
                        vb = segs_b[si]["vb"]
                        if si + 1 < len(segs_b):
                            seg_load(si + 1)
                        attn_unit(ktb[s_], b_kt[s_], [(vtb[vb[0]], b_vt[vb[0]]), (vtb[vb[1]], b_vt[vb[1]])], ch0 + c)
                        for hf in range(2):
                            P.op("vector", (lambda c=c, hf=hf: lambda e: e.tensor_tensor(
                                out=ocn[c][hf][:, 0:GTg], in0=psO[hf][:, 0:GTg], in1=rsb[:, 0:GTg],
                                op=ALU.mult))(), reads=[b_psO[hf], b_rsb], writes=[b_ocn[c][hf]])
                    for hf in range(2):
                        P.op("vector", (lambda hf=hf: lambda e: e.scalar_tensor_tensor(
                            out=dbuf[hf][:, 0:GTg], in0=ocn[1][hf][:, 0:GTg], scalar=neglam,
                            in1=ocn[0][hf][:, 0:GTg], op0=ALU.mult, op1=ALU.add))(),
                            reads=[b_ocn[1][hf], b_ocn[0][hf], bc], writes=[b_d[hf]])
                        P.op("vector", (lambda hf=hf: lambda e: e.tensor_tensor(
                            out=sqd[hf][:, 0:GTg], in0=dbuf[hf][:, 0:GTg], in1=dbuf[hf][:, 0:GTg],
                            op=ALU.mult))(), reads=[b_d[hf]], writes=[b_sqd[hf]])

                    rs_ = cnt["kb"] % NSS
                    cnt["kb"] += 1
                    psR, b_psR = psS[rs_], b_psS[rs_]

                    def rmm(e, psR=psR):
                        e.matmul(psR[:, 0:GTg], lhsT=ones32[:], rhs=sqd[0][:, 0:GTg], start=True, stop=False)
                        return e.matmul(psR[:, 0:GTg], lhsT=ones32[:], rhs=sqd[1][:, 0:GTg], start=False, stop=True)
                    P.op("tensor", rmm, reads=[b_sqd[0], b_sqd[1], bc], writes=[b_psR])
                    P.op("vector", (lambda psR=psR: lambda e: e.tensor_scalar(
                        out=rbuf[0][:, 0:GTg], in0=psR[:, 0:GTg], scalar1=1.0 / 256, scalar2=EPS, op0=ALU.mult,
                        op1=ALU.add))(), reads=[b_psR], writes=[b_rb[0]])
                    P.op("scalar", lambda e: e.activation(out=rbuf[0][:, 0:GTg], in_=rbuf[0][:, 0:GTg],
                                                          func=AF.Sqrt), reads=[b_rb[0]], writes=[b_rb[0]])
                    P.op("vector", lambda e: e.reciprocal(out=rbuf[1][:, 0:GTg], in_=rbuf[0][:, 0:GTg]),
                         reads=[b_rb[0]], writes=[b_rb[1]])
                    for hf in range(2):
                        oi = cnt["ob"] % 2
                        cnt["ob"] += 1
                        P.op("vector", (lambda hf=hf, oi=oi: lambda e: e.scalar_tensor_tensor(
                            out=obuf[oi][:, 0:GTg], in0=dbuf[hf][:, 0:GTg], scalar=sgcol[:, hf:hf + 1],
                            in1=rbuf[1][:, 0:GTg], op0=ALU.mult, op1=ALU.mult))(),
                            reads=[b_d[hf], b_rb[1], bc], writes=[b_ob[oi]])
                        P.op("vector", (lambda hf=hf, oi=oi, hc=hc: lambda e: e.tensor_tensor(
                            out=mixT[:, A_H + NBC + 2 * hc + hf, 0:GTg], in0=obuf[oi][:, 0:GTg],
                            in1=gateT[:, A_H + 2 * hc + hf, 0:GTg], op=ALU.mult))(),
                            reads=[b_ob[oi], b_gate[A_H + 2 * hc + hf]], writes=[b_mix[A_H + NBC + 2 * hc + hf]])
                if DEBUG and gi_ == 0:
                    P.dma("sync", lambda e: e.dma_start(out=dram["dbgB"][:, :, :], in_=mixT[:]), "dbg", reads=b_mix)
                P.emit("B")

            with ExitStack() as sC:
                gbc = L.sb([128, D], F32, sC, "gbc")
                b_gbc = Buf("gbc")
                P.chan("gbc")
                hres = [L.sb([128, WT], F32, sC, "hres") for _ in range(2)]
                b_hres = [Buf("hres0"), Buf("hres1")]
                ores = [L.sb([128, WT], F32, sC, "ores") for _ in range(2)]
                b_ores = [Buf("ores0"), Buf("ores1")]
                psC = [L.ps([128, WT], F32, sC, "psC") for _ in range(2)]
                b_psC = [Buf("psC0"), Buf("psC1")]
                junkc = L.sb([128, WT], F32, sC, "junkc")
                b_junkc = Buf("junkc")
                for nm_ in ("hres0", "hres1", "ores0", "ores1"):
                    P.chan(nm_)
                P.dma("sync", lambda e: e.dma_start(out=gbc[:], in_=dram["gate_row"][kind:kind + 1, :]
                                                    .partition_broadcast(128)), "gbc", writes=[b_gbc])
                ccount = 0
                for c0 in range(0, D, WT):
                    wt, bw = L.next_w()
                    for ti in range(len(tiles)):
                        i = ccount % 2
                        ccount += 1
                        r0 = g0 + ti * 128
                        rd = d0 + ti * 128

                        def mm(e, wt=wt, ti=ti, i=i):
                            for k in range(NCHUNK):
                                ins = e.matmul(psC[i][:, 0:WT], lhsT=mixT[:, k, ti * 128:(ti + 1) * 128],
                                               rhs=wt[:, k, :], start=(k == 0), stop=(k == NCHUNK - 1))
                            return ins
                        P.op("tensor", mm, reads=b_mix + [bw], writes=[b_psC[i]])
                        P.dma("sync", (lambda i=i, r0=r0, c0=c0: lambda e: e.dma_start(
                            out=hres[i][:, :], in_=src[r0:r0 + 128, c0:c0 + WT]))(), "hres%d" % i,
                            writes=[b_hres[i]])
                        P.op("vector", (lambda i=i, c0=c0: lambda e: e.tensor_tensor(
                            out=ores[i][:, :], in0=psC[i][:, 0:WT], in1=gbc[:, c0:c0 + WT], op=ALU.mult))(),
                            reads=[b_psC[i], b_gbc], writes=[b_ores[i]])
                        P.op("vector", (lambda i=i: lambda e: e.tensor_tensor(
                            out=ores[i][:, :], in0=ores[i][:, :], in1=hres[i][:, :], op=ALU.add))(),
                            reads=[b_ores[i], b_hres[i]], writes=[b_ores[i]])
                        if last:
                            tg = ssq_t0 + ti
                            P.op("vector", (lambda i=i: lambda e: e.tensor_tensor(
                                out=junkc[:, :], in0=ores[i][:, :], in1=ores[i][:, :], op=ALU.mult))(),
                                reads=[b_ores[i]], writes=[b_junkc])
                            P.op("vector", (lambda tg=tg, c0=c0: lambda e: e.tensor_reduce(
                                out=ssq[:, tg, c0 // WT:c0 // WT + 1], in_=junkc[:, :], axis=AX.X, op=ALU.add))(),
                                reads=[b_junkc], writes=[b_ssq])
                        P.dma("sync", (lambda i=i, rd=rd, c0=c0: lambda e: e.dma_start(
                            out=dst[rd:rd + 128, c0:c0 + WT], in_=ores[i][:, :]))(), "ores%d" % i,
                            reads=[b_ores[i]])
                P.emit("C")

        if last:
            with ExitStack() as sF:
                fgb = L.sb([128, D], F32, sF, "fgb")
                b_fgb = Buf("fgb")
                P.chan("fgb"); P.chan("f0"); P.chan("f1"); P.chan("fo0"); P.chan("fo1")
                fb = [L.sb([128, D], F32, sF, "fb") for _ in range(2)]
                b_fb = [Buf("fb0"), Buf("fb1")]
                fst = L.sb([128, TOK // 128, 4], F32, sF, "fst")
                b_fst = Buf("fst")
                P.dma("sync", lambda e: e.dma_start(out=fgb[:], in_=dram["fg_row"].partition_broadcast(128)),
                      "fgb", writes=[b_fgb])
                NTL = TOK // 128
                P.op("vector", lambda e: e.tensor_reduce(out=fst[:, :, 0], in_=ssq[:], axis=AX.X, op=ALU.add),
                     reads=[b_ssq], writes=[b_fst])
                P.op("vector", lambda e: e.tensor_scalar(out=fst[:, :, 1], in0=fst[:, :, 0], scalar1=1.0 / D,
                                                         scalar2=EPS, op0=ALU.mult, op1=ALU.add),
                     reads=[b_fst], writes=[b_fst])
                P.op("gpsimd", lambda e: e.tensor_tensor(out=fst[:, :, 2], in0=fst[:, :, 1],
                                                         in1=mhalfw[:, 0:NTL], op=ALU.pow),
                     reads=[b_fst, bc], writes=[b_fst])
                for t in range(NTL):
                    i = t % 2
                    P.dma("sync", (lambda i=i, t=t: lambda e: e.dma_start(
                        out=fb[i][:], in_=h2[t * 128:(t + 1) * 128, :]))(), "f%d" % i, writes=[b_fb[i]])
                    P.op("vector", (lambda i=i, t=t: lambda e: e.scalar_tensor_tensor(
                        out=fb[i][:], in0=fb[i][:], scalar=fst[:, t, 2:3], in1=fgb[:], op0=ALU.mult,
                        op1=ALU.mult))(), reads=[b_fb[i], b_fst, b_fgb], writes=[b_fb[i]])
                    P.dma("sync", (lambda i=i, t=t: lambda e: e.dma_start(
                        out=dram["ho"][t * 128:(t + 1) * 128, :], in_=fb[i][:]))(), "fo%d" % i, reads=[b_fb[i]])
                P.emit("final")
    return nc


class _F:
    pass


def build_fused(cfg, GT=512):
    from contextlib import ExitStack
    nc = bass.Bass("TRN2", target_bir_lowering=False)
    D, KC, TOK, CTX, SEQ = cfg.D, cfg.KC, cfg.TOK, cfg.CTX, cfg.SEQ
    SK = CTX + SEQ
    LYR = cfg.DEPTH
    NS = RANKS
    NG = TOK // GT
    assert NG * GT == TOK
    HN = 2 * RANKS + 2 * (NG - 1)
    NBC = cfg.B_W // 128
    A_KV, C_H = cfg.A_KV, cfg.C_H
    WM = 512
    dram = {}

    def din(name, shape, dt=F32):
        dram[name] = nc.dram_tensor(name, list(shape), dt, kind="ExternalInput").ap()

    def scratch(name, shape, dt=F32):
        dram[name] = nc.dram_tensor(name, list(shape), dt).ap()

    din("x", [SEQ, D]); din("ctx", [CTX, D]); din("ident", [128, 128]); din("cT", [128, KC, 2])
    din("wm", [LYR, D, 3 * D]); din("bm_col", [128, LYR, 3 * KC]); din("ng_col", [128, LYR, KC])
    din("kn", [LYR, 128]); din("qn", [LYR, 128]); din("convw", [128, LYR, NBC, 3])
    din("lam4", [LYR, 4, 128]); din("sg_col", [128, LYR, 2])
    din("cs", [SEQ, 64]); din("sn", [SEQ, 64]); din("sel", [1, NS * NG * 2 * HN]); din("fg_row", [1, D])
    din("w_in", [LYR, D, cfg.IN]); din("w_out", [LYR, D, D])
    dram["out"] = nc.dram_tensor("out", [TOK, D], F32, kind="ExternalOutput").ap()
    scratch("mcol_d", [LYR, 128, 2, 3, KC]); scratch("grow_d", [LYR, 2, D])
    scratch("ktA_d", [A_KV, 128, SK], BF16); scratch("vA_d", [SK, cfg.VA], BF16)
    scratch("ktC_d", [2 * C_H, 128, SK], BF16); scratch("vC_d", [SK, cfg.VC], BF16)
    scratch("h1_d", [SEQ, D]); scratch("hc1_d", [CTX, D]); scratch("h2_d", [TOK, D])

    with ExitStack() as st:
        P = Prog(nc, st)
        with ExitStack() as sm:
            Lm = Layer(cfg, nc, sm, P, GT)
            c32 = Lm.sb([128, KC, 2], F32, name="c32")
            cb = Lm.sb([128, KC, 2], BF16, name="cb")
            bmcol = Lm.sb([128, LYR, 3 * KC], F32, name="bmcol")
            id32 = Lm.sb([128, 128], F32, name="id32")
            mcolT = [Lm.sb([128, 2, 3, KC], F32, name="mcolT") for _ in range(LYR)]
            growT = Lm.sb([KC, 2, 128], F32, name="growT")
            b_c, b_m, b_g = Buf("c"), [Buf("m%d" % l) for l in range(LYR)], Buf("growT")
            psM = [Lm.ps([128, 8], F32, sm, "psM") for _ in range(2)]
            b_psM = [Buf("psM0"), Buf("psM1")]
            psG = Lm.ps([128, 128], F32, sm, "psG")
            b_psG = Buf("psG")
            P.chan("mc"); P.chan("mo")
            for o, i in ((c32[:], dram["cT"]), (bmcol[:], dram["bm_col"]), (id32[:], dram["ident"])):
                P.dma("sync", (lambda o=o, i=i: lambda e: e.dma_start(out=o, in_=i))(), "mc", writes=[b_c])
            P.op("scalar", lambda e: e.activation(out=cb[:], in_=c32[:], func=AF.Silu), reads=[b_c], writes=[b_c])
            Lm.alloc_wring(WM, 2)
            for l in range(LYR):
                for c0 in range(0, 3 * D, WM):
                    Lm.wsched.append(dram["wm"][l][:, c0:c0 + WM])
            tcount = 0
            for l in range(LYR):
                for c0 in range(0, 3 * D, WM):
                    wt, bw = Lm.next_w()
                    i = tcount % 2
                    tcount += 1

                    def mm(e, wt=wt, i=i):
                        for blk in range(WM // 128):
                            for k in range(KC):
                                ins = e.matmul(psM[i][:, blk * 2:blk * 2 + 2], lhsT=wt[:, k, blk * 128:(blk + 1) * 128],
                                               rhs=cb[:, k, :], start=(k == 0), stop=(k == KC - 1))
                        return ins
                    P.op("tensor", mm, reads=[bw, b_c], writes=[b_psM[i]])

                    def ev(e, l=l, c0=c0, i=i):
                        for blk in range(WM // 128):
                            cblk = c0 // 128 + blk
                            ins = e.tensor_scalar(out=mcolT[l][:, :, cblk // KC, cblk % KC],
                                                  in0=psM[i][:, blk * 2:blk * 2 + 2],
                                                  scalar1=bmcol[:, l, cblk:cblk + 1], scalar2=None, op0=ALU.add)
                        return ins
                    P.op("vector", ev, reads=[b_psM[i], b_c], writes=[b_m[l]])
                P.dma("sync", (lambda l=l: lambda e: e.dma_start(out=dram["mcol_d"][l], in_=mcolT[l][:]))(), "mo",
                      reads=[b_m[l]])
                for r in range(2):
                    P.op("tensor", (lambda l=l, r=r: lambda e: e.transpose(
                        out=psG[0:KC, 0:128], in_=mcolT[l][:, r, 2, :], identity=id32[:]))(),
                        reads=[b_m[l], b_c], writes=[b_psG])
                    P.op("scalar", (lambda r=r: lambda e: e.activation(out=growT[:, r, :], in_=psG[0:KC, 0:128],
                                                                       func=AF.Copy))(), reads=[b_psG], writes=[b_g])
                P.dma("sync", (lambda l=l: lambda e: e.dma_start(
                    out=dram["grow_d"][l].rearrange("r (k p) -> k r p", p=128), in_=growT[:]))(), "mo", reads=[b_g])
            P.emit()

        tiles_ctx = [128] * (CTX // 128)
        tiles_g = [128] * (GT // 128)
        for l in range(LYR):
            last = (l == LYR - 1)
            hsrc = dram["x"] if l == 0 else dram["h1_d"]
            hcsrc = dram["ctx"] if l == 0 else dram["hc1_d"]
            F = _F()
            F.nc, F.P = nc, P
            F.nsel = NS * NG * 2 * HN
            F.dram = {
                "ident": dram["ident"], "mcol": dram["mcol_d"][l], "ng_col": dram["ng_col"][:, l, :],
                "cs": dram["cs"], "sn": dram["sn"], "kn": dram["kn"][l:l + 1, :],
                "wkv": dram["w_in"][l][:, 0:cfg.KV],
                "ktA": dram["ktA_d"], "vA": dram["vA_d"], "ktC": dram["ktC_d"], "vC": dram["vC_d"],
                "gate_row": dram["grow_d"][l], "qn": dram["qn"][l:l + 1, :], "convw": dram["convw"][:, l],
                "lam4": dram["lam4"][l], "sg_col": dram["sg_col"][:, l, :], "sel": dram["sel"],
                "wq": dram["w_in"][l][:, cfg.KV:cfg.IN], "wo": dram["w_out"][l],
                "h2": dram["h2_d"], "ho": dram["out"], "fg_row": dram["fg_row"],
            }
            F.kv_groups = [(1, hcsrc, 0, tiles_ctx, 0, None)]
            for i in range(NS):
                for g in range(NG):
                    r0 = i * TOK + g * GT
                    F.kv_groups.append((0, hsrc, r0, tiles_g, CTX + r0, r0))
            build_kv(cfg, GT, F)
            F.rest_groups = []
            for i in range(NS if not last else 1):
                for g in range(NG):
                    r0 = i * TOK + g * GT
                    halo = None
                    if g == 0:
                        halo = []
                        for r in range(RANKS):
                            halo += [(hsrc, r * TOK, 1), (hsrc, r * TOK + TOK - 1, 1)]
                        for gg in range(1, NG):
                            halo.append((hsrc, i * TOK + gg * GT - 1, 2))
                    dst, d0 = (dram["h2_d"], g * GT) if last else (dram["h1_d"], r0)
                    F.rest_groups.append((0, hsrc, r0, tiles_g, dst, d0, r0, (i * NG + g) * 2 * HN,
                                          (g * GT) // 128, halo))
            if not last:
                F.rest_groups.append((1, hcsrc, 0, tiles_ctx, dram["hc1_d"], 0, None, None, None, None))
            build_rest(cfg, l, last, GT, F)
    return nc


def run_fused(cfg, I, GT=512):
    nc = build_fused(cfg, GT)
    D, KC, TOK = cfg.D, cfg.KC, cfg.TOK
    NG = TOK // GT
    HN = 2 * RANKS + 2 * (NG - 1)
    NBC = cfg.B_W // 128
    LYR = cfg.DEPTH
    cs, sn = rope_tables(cfg)
    shared = {
        "ident": np.eye(128, dtype=np.float32),
        "wm": np.ascontiguousarray(I["w_mod"], dtype=np.float32),
        "bm_col": np.ascontiguousarray(col_layout(I["b_mod"], 3 * KC)),
        "ng_col": np.ascontiguousarray(col_layout(I["norm_g"], KC)),
        "kn": np.ascontiguousarray(I["k_norm_a"], dtype=np.float32),
        "qn": np.ascontiguousarray(I["q_norm_a"], dtype=np.float32),
        "convw": np.ascontiguousarray(
            np.transpose(I["conv_w"], (0, 2, 1)).reshape(LYR, NBC, 128, 3).transpose(2, 0, 1, 3), dtype=np.float32),
        "lam4": np.ascontiguousarray(np.stack([I["lambda_q1"], I["lambda_k1"], I["lambda_q2"], I["lambda_k2"]], 1),
                                     dtype=np.float32),
        "sg_col": np.ascontiguousarray(I["subln_g"].reshape(LYR, 2, 128).transpose(2, 0, 1), dtype=np.float32),
        "fg_row": np.ascontiguousarray(I["final_g"].reshape(1, D), dtype=np.float32),
        "w_in": np.ascontiguousarray(I["w_in"], dtype=np.float32),
        "w_out": np.ascontiguousarray(I["w_out"], dtype=np.float32),
    }
    in_maps = []
    for r in range(N_CORES):
        b, j = divmod(r, RANKS)
        order = [(j + i) % RANKS for i in range(RANKS)]
        idx = np.concatenate([np.arange(o * TOK, (o + 1) * TOK) for o in order])
        rows = np.stack([I["c"][b], I["c_ctx"]], axis=-1)
        sel = np.zeros((RANKS, NG, 2, HN), np.float32)
        for i in range(RANKS):
            rk = order[i]
            for g in range(NG):
                if g == 0:
                    if rk > 0:
                        sel[i, g, 0, 2 * ((i - 1) % RANKS) + 1] = 1.0
                else:
                    sel[i, g, 0, 2 * RANKS + 2 * (g - 1)] = 1.0
                if g == NG - 1:
                    if rk < RANKS - 1:
                        sel[i, g, 1, 2 * ((i + 1) % RANKS)] = 1.0
                else:
                    sel[i, g, 1, 2 * RANKS + 2 * g + 1] = 1.0
        d = dict(shared)
        d.update({
            "x": np.ascontiguousarray(I["x"][b][idx]),
            "ctx": np.ascontiguousarray(I["ctx"][b]),
            "cT": np.ascontiguousarray(rows.reshape(KC, 128, 2).transpose(1, 0, 2)).astype(np.float32),
            "cs": np.ascontiguousarray(cs[idx]), "sn": np.ascontiguousarray(sn[idx]),
            "sel": sel.reshape(1, -1),
        })
        in_maps.append(d)
    res = run_bass_kernel_spmd(nc, in_maps, core_ids=list(range(N_CORES))).results
    out = np.empty((cfg.BATCH, cfg.SEQ, D), np.float32)
    for r in range(N_CORES):
        b, j = divmod(r, RANKS)
        out[b, j * TOK:(j + 1) * TOK] = res[r]["out"]
    return out


def halo_rows(cfg, hb, j, GT):
    T = cfg.TOK
    rows = []
    for r in range(RANKS):
        rows += [hb[r * T], hb[r * T + T - 1]]
    NG = (T + GT - 1) // GT
    for g in range(1, NG):
        rows += [hb[j * T + g * GT - 1], hb[j * T + g * GT]]
    return np.ascontiguousarray(np.stack(rows, 0)).astype(np.float32)


def sel_vectors(cfg, j, GT):
    T = cfg.TOK
    NG = (T + GT - 1) // GT
    HN = 2 * RANKS + 2 * (NG - 1)
    sel = np.zeros((NG, 2, HN), np.float32)
    for g in range(NG):
        if g == 0:
            if j > 0:
                sel[g, 0, 2 * (j - 1) + 1] = 1.0
        else:
            sel[g, 0, 2 * RANKS + 2 * (g - 1)] = 1.0
        if g == NG - 1:
            if j < RANKS - 1:
                sel[g, 1, 2 * (j + 1)] = 1.0
        else:
            sel[g, 1, 2 * RANKS + 2 * g + 1] = 1.0
    return sel.reshape(1, -1)


def run_rest(cfg, l, last, hs, hcs, m, kv, P_, cs, sn, GT=512):
    nc = build_rest(cfg, l, last, GT)
    D = cfg.D
    wq = np.ascontiguousarray(P_["w_in"][l][:, cfg.KV:])
    wo = np.ascontiguousarray(P_["w_out"][l])
    NBC = cfg.B_W // 128
    convw = np.ascontiguousarray(P_["conv_w"][l].T.reshape(NBC, 128, 3).transpose(1, 0, 2)).astype(np.float32)
    lam4 = np.stack([P_["lambda_q1"][l], P_["lambda_k1"][l], P_["lambda_q2"][l], P_["lambda_k2"][l]], 0)
    sg_col = np.ascontiguousarray(P_["subln_g"][l].reshape(2, 128).T).astype(np.float32)
    in_maps = []
    for r in range(N_CORES):
        b, j = divmod(r, RANKS)
        d = common_inputs(cfg, l, b, j, hs, hcs, m, P_["norm_g"], cs, sn)
        d["hh"] = halo_rows(cfg, hs[b], j, GT)
        d["sel"] = sel_vectors(cfg, j, GT)
        d["gate_row"] = np.ascontiguousarray(m[b][l][:, 2 * D:3 * D]).astype(np.float32)
        d["qn"] = np.ascontiguousarray(P_["q_norm_a"][l].reshape(1, 128)).astype(np.float32)
        d["convw"] = convw
        d["lam4"] = np.ascontiguousarray(lam4).astype(np.float32)
        d["sg_col"] = sg_col
        d["wq"] = wq
        d["wo"] = wo
        d.update(kv[b])
        if last:
            d["fg_row"] = np.ascontiguousarray(P_["final_g"].reshape(1, D)).astype(np.float32)
        in_maps.append(d)
    res = run_bass_kernel_spmd(nc, in_maps, core_ids=list(range(N_CORES))).results
    if DEBUG:
        run_rest.dbg = res
    hs2 = [np.concatenate([res[b * RANKS + j]["ho"] for j in range(RANKS)], 0) for b in range(cfg.BATCH)]
    hcs2 = None if last else [res[b * RANKS]["hco"] for b in range(cfg.BATCH)]
    return hs2, hcs2


def forward(cfg, inputs, GT=512):
    I = {k: np.asarray(v) for k, v in inputs.items()}
    cs, sn = rope_tables(cfg)
    m = run_mod(cfg, I["c"], I["c_ctx"], I["w_mod"], I["b_mod"])
    hs = [I["x"][b] for b in range(cfg.BATCH)]
    hcs = [I["ctx"][b] for b in range(cfg.BATCH)]
    for l in range(cfg.DEPTH):
        last = (l == cfg.DEPTH - 1)
        kv = run_kv(cfg, l, hs, hcs, m, I["norm_g"], I["k_norm_a"], I["w_in"], cs, sn, GT)
        hs, hcs2 = run_rest(cfg, l, last, hs, hcs, m, kv, I, cs, sn, GT)
        if not last:
            hcs = hcs2
    return np.stack(hs, 0).astype(np.float32)


MODE = "fused"


def kernel(**inputs):
    cfg = Cfg()
    if MODE == "fused":
        return run_fused(cfg, {k: np.asarray(v) for k, v in inputs.items()})
    return forward(cfg, inputs)
```

```python
import math
import numpy as np
import concourse.bass as bass
import concourse.mybir as mybir
from concourse.bass_utils import run_bass_kernel_spmd

F32 = mybir.dt.float32
BF16 = mybir.dt.bfloat16
AF = mybir.ActivationFunctionType
ALU = mybir.AluOpType
AX = mybir.AxisListType

N_CORES = 8
RANKS = 4


class Cfg:
    def __init__(self, d_model=4096, seq=4096, ctx_len=256, depth=2, batch=2):
        self.D = d_model
        self.SEQ = seq
        self.CTX = ctx_len
        self.DEPTH = depth
        self.BATCH = batch
        self.KC = d_model // 128
        self.TOK = seq // RANKS
        self.HD = 128
        self.A_W = d_model // 2
        self.A_H = self.A_W // 128
        self.A_KV = self.A_H // 4
        self.B_W = d_model // 4
        self.C_W = d_model // 4
        self.C_H = self.C_W // 256
        self.KA = self.A_KV * 128
        self.VA = self.A_KV * 128
        self.KCC = self.C_H * 256
        self.VC = self.C_W
        self.KV = self.KA + self.VA + self.KCC + self.VC
        self.IN = self.KV + self.A_W + self.C_H * 256 + 3 * self.B_W + self.A_W + self.B_W + self.C_W
        self.MODC = 3 * d_model // RANKS
        self.WT = min(512, self.KA)


def build_mod(cfg):
    nc = bass.Bass("TRN2", target_bir_lowering=False)
    KC, MODC, L = cfg.KC, cfg.MODC, cfg.DEPTH
    WT = 512 if MODC % 512 == 0 else 256
    NT = MODC // WT
    cT = nc.dram_tensor("cT", [128, KC, 2], F32, kind="ExternalInput").ap()
    wm = nc.dram_tensor("wm", [L, cfg.D, MODC], F32, kind="ExternalInput").ap()
    bm = nc.dram_tensor("bm", [L, MODC], F32, kind="ExternalInput").ap()
    mo = nc.dram_tensor("mo", [L, 2, MODC], F32, kind="ExternalOutput").ap()
    with (
        nc.sbuf_tensor("c32", [128, KC, 2], F32) as c32,
        nc.sbuf_tensor("cb", [128, KC, 2], BF16) as cb,
        nc.sbuf_tensor("w0", [128, KC, WT], BF16) as w0,
        nc.sbuf_tensor("w1", [128, KC, WT], BF16) as w1,
        nc.sbuf_tensor("bb", [2, L, MODC], F32) as bb,
        nc.sbuf_tensor("res", [2, L, MODC], F32) as res,
        nc.psum_tensor("p0", [2, WT], F32) as p0,
        nc.psum_tensor("p1", [2, WT], F32) as p1,
        nc.semaphore("s_in") as s_in,
        nc.semaphore("s_act") as s_act,
        nc.semaphore("s_w0") as s_w0,
        nc.semaphore("s_w1") as s_w1,
        nc.semaphore("s_pe") as s_pe,
        nc.semaphore("s_ev") as s_ev,
        nc.semaphore("s_out") as s_out,
        nc.Block() as block,
    ):
        ws = [w0, w1]
        ps = [p0, p1]
        s_w = [s_w0, s_w1]
        tiles = [(l, t) for l in range(L) for t in range(NT)]

        @block.sync
        def _(s):
            s.dma_start(out=c32[:], in_=cT[:, :, :]).then_inc(s_in, 16)
            for r in range(2):
                s.dma_start(out=bb[r:r + 1, :, :], in_=bm.rearrange("(o l) m -> o l m", o=1)).then_inc(s_in, 16)

        @block.scalar
        def _(a):
            a.wait_ge(s_in, 48)
            a.activation(out=cb[:], in_=c32[:], func=AF.Silu).then_inc(s_act, 1)

        @block.gpsimd
        def _(g):
            for i, (l, t) in enumerate(tiles):
                if i >= 2:
                    g.wait_ge(s_pe, i - 1)
                g.dma_start(
                    out=ws[i % 2][:],
                    in_=wm[l].rearrange("(kc p) n -> p kc n", p=128)[:, :, t * WT:(t + 1) * WT],
                ).then_inc(s_w[i % 2], 16)
            g.wait_ge(s_ev, len(tiles))
            for l in range(L):
                g.dma_start(out=mo[l], in_=res[:, l, :]).then_inc(s_out, 16)
            g.wait_ge(s_out, 16 * L)

        @block.tensor
        def _(pe):
            pe.wait_ge(s_act, 1)
            for i, (l, t) in enumerate(tiles):
                pe.wait_ge(s_w[i % 2], 16 * (i // 2 + 1))
                if i >= 2:
                    pe.wait_ge(s_ev, i - 1)
                for k in range(KC):
                    ins = pe.matmul(ps[i % 2][:], lhsT=cb[:, k, :], rhs=ws[i % 2][:, k, :],
                                    start=(k == 0), stop=(k == KC - 1))
                ins.then_inc(s_pe, 1)

        @block.vector
        def _(v):
            v.wait_ge(s_in, 48)
            for i, (l, t) in enumerate(tiles):
                v.wait_ge(s_pe, i + 1)
                v.tensor_tensor(out=res[:, l, t * WT:(t + 1) * WT], in0=ps[i % 2][:],
                                in1=bb[:, l, t * WT:(t + 1) * WT], op=ALU.add).then_inc(s_ev, 1)
    return nc


def run_mod(cfg, c, c_ctx, w_mod, b_mod):
    nc = build_mod(cfg)
    in_maps = []
    for r in range(N_CORES):
        b, j = divmod(r, RANKS)
        rows = np.stack([c[b], c_ctx], axis=-1)
        cT = np.ascontiguousarray(rows.reshape(cfg.KC, 128, 2).transpose(1, 0, 2))
        sl = slice(j * cfg.MODC, (j + 1) * cfg.MODC)
        in_maps.append({
            "cT": cT.astype(np.float32),
            "wm": np.ascontiguousarray(w_mod[:, :, sl]),
            "bm": np.ascontiguousarray(b_mod[:, sl]),
        })
    res = run_bass_kernel_spmd(nc, in_maps, core_ids=list(range(N_CORES)))
    out = []
    for b in range(cfg.BATCH):
        out.append(np.concatenate([res.results[b * RANKS + j]["mo"] for j in range(RANKS)], axis=-1))
    return out


class Tok:
    __slots__ = ("sem", "val", "eng")

    def __init__(self, sem, val, eng):
        self.sem, self.val, self.eng = sem, val, eng


class Buf:
    __slots__ = ("name", "w", "r")

    def __init__(self, name):
        self.name, self.w, self.r = name, None, []


class Prog:
    ENGS = ("tensor", "vector", "scalar", "gpsimd", "sync")

    def __init__(self, nc, stack):
        self.nc, self.stack = nc, stack
        self.sem = {e: stack.enter_context(nc.semaphore("sem_" + e)) for e in self.ENGS[:4]}
        self.cnt = {e: 0 for e in self.ENGS[:4]}
        self.pending = {e: [] for e in self.ENGS}
        self.waited = {e: {} for e in self.ENGS}
        self.chans = {}
        self.nops = 0

    def chan(self, name):
        if name not in self.chans:
            self.chans[name] = [self.stack.enter_context(self.nc.semaphore("ch_" + name)), 0]
        return name

    def _waits(self, eng, deps):
        out = []
        w = self.waited[eng]
        for t in deps:
            if t is None:
                continue
            key = id(t.sem)
            if w.get(key, 0) < t.val:
                w[key] = t.val
                out.append((t.sem, t.val))
        return out

    def _deps(self, eng, reads, writes, is_dma):
        deps = []
        for b in reads:
            deps.append(b.w)
        for b in writes:
            if b.w is not None and (is_dma or b.w.eng != eng):
                deps.append(b.w)
            for t in b.r:
                if is_dma or t.eng != eng:
                    deps.append(t)
        return deps

    def _commit(self, tok, reads, writes):
        for b in reads:
            b.r.append(tok)
        for b in writes:
            b.w, b.r = tok, []

    def op(self, eng, fn, reads=(), writes=()):
        waits = self._waits(eng, self._deps(eng, reads, writes, False))
        self.cnt[eng] += 1
        tok = Tok(self.sem[eng], self.cnt[eng], eng)
        self.pending[eng].append((waits, fn, (self.sem[eng], 1)))
        self._commit(tok, reads, writes)
        self.nops += 1
        return tok

    def dma(self, queue, fn, chan, reads=(), writes=()):
        waits = self._waits(queue, self._deps(queue, reads, writes, True))
        c = self.chans[chan]
        c[1] += 16
        tok = Tok(c[0], c[1], None)
        self.pending[queue].append((waits, fn, (c[0], 16)))
        self._commit(tok, reads, writes)
        self.nops += 1
        return tok

    SCOPES = False

    def emit(self, scope=None):
        if scope and Prog.SCOPES:
            self.nscope = getattr(self, "nscope", 0) + 1
            with self.nc.named_scope("%03d_%s" % (self.nscope, scope)):
                return self.emit()
        finals = [(self.sem[e], self.cnt[e]) for e in self.ENGS[:4] if self.cnt[e] > 0]
        finals += [(c[0], c[1]) for c in self.chans.values() if c[1] > 0]

        def runner(eng, ops):
            def run(e):
                for waits, fn, inc in ops:
                    for s, v in waits:
                        e.wait_ge(s, v)
                    ins = fn(e)
                    ins.then_inc(inc[0], inc[1])
                w = self.waited[eng]
                for s, v in finals:
                    if w.get(id(s), 0) < v and not (eng in self.sem and s is self.sem[eng]):
                        e.wait_ge(s, v)
                        w[id(s)] = v
            return run

        with self.nc.Block() as block:
            for eng in self.ENGS:
                ops = self.pending[eng]
                getattr(block, eng)(runner(eng, ops))
                self.pending[eng] = []


def _bc(base, dims):
    return bass.AP(base.tensor, base.offset, [list(base.ap[0])] + [list(d) for d in dims])


EPS = 1e-6
ATTN_SCALE = 1.0 / math.sqrt(128.0)


class Layer:
    UID = 0

    def __init__(self, cfg, nc, stack, P, GT):
        self.cfg, self.nc, self.st, self.P, self.GT = cfg, nc, stack, P, GT
        self.uid = 0

    def sb(self, shape, dt, stack=None, name=None):
        Layer.UID += 1
        return (stack or self.st).enter_context(
            self.nc.sbuf_tensor("%s_%d" % (name or "t", Layer.UID), list(shape), dt))

    def ps(self, shape, dt, stack=None, name=None):
        Layer.UID += 1
        full = [128, 512] if dt == F32 else [128, 1024]
        assert shape[0] <= 128 and shape[1] <= full[1]
        return (stack or self.st).enter_context(
            self.nc.psum_tensor("%s_%d" % (name or "p", Layer.UID), full, dt))

    def load_consts(self, dram):
        cfg, P, nc = self.cfg, self.P, self.nc
        KC = cfg.KC
        P.chan("const")
        self.ident32 = self.sb([128, 128], F32, name="ident32")
        self.identb = self.sb([128, 128], BF16, name="identb")
        self.mcol = self.sb([128, 2, 3, KC], F32, name="mcol")
        self.ngcol = self.sb([128, KC], F32, name="ngcol")
        self.gscol = self.sb([128, 2, KC], F32, name="gscol")
        self.mhalf = self.sb([128, 8], F32, name="mhalf")
        self.cs = self.sb([128, self.GT // 128, 64], F32, name="cs")
        self.sn = self.sb([128, self.GT // 128, 64], F32, name="sn")
        self.dram = dram
        P.chan("rope")
        self.b_const = Buf("const")
        bc = self.b_const
        loads = [
            (self.ident32[:], dram["ident"][:, :]),
            (self.mcol[:], dram["mcol"]),
            (self.ngcol[:], dram["ng_col"]),
        ]
        for o, i in loads:
            P.dma("sync", (lambda o=o, i=i: lambda e: e.dma_start(out=o, in_=i))(), "const", writes=[bc])
        self.extra_const_loads(dram, bc)
        P.op("gpsimd", lambda e: e.memset(self.mhalf[:], -0.5), writes=[bc])
        P.op("vector", lambda e: e.tensor_copy(out=self.identb[:], in_=self.ident32[:]), reads=[bc], writes=[bc])
        for r in range(2):
            P.op("vector", (lambda r=r: lambda e: e.scalar_tensor_tensor(
                out=self.gscol[:, r, :], in0=self.mcol[:, r, 1, :], scalar=1.0, in1=self.ngcol[:],
                op0=ALU.add, op1=ALU.mult))(), reads=[bc], writes=[bc])

    def extra_const_loads(self, dram, bc):
        pass

    def load_rope(self, row0, ntiles):
        for t, nm in ((self.cs, "cs"), (self.sn, "sn")):
            src = self.dram[nm][row0:row0 + 128 * ntiles, :].rearrange("(t p) c -> p t c", p=128)
            self.P.dma("sync", (lambda t=t, src=src: lambda e: e.dma_start(out=t[:, 0:ntiles, :], in_=src))(),
                       "rope", writes=[self.b_const])

    def alloc_norm(self, stack):
        cfg = self.cfg
        self.hbuf = [self.sb([128, cfg.D], F32, stack, "hbuf") for _ in range(2)]
        self.b_h = [Buf("h0"), Buf("h1")]
        self.junk = self.sb([128, min(cfg.D, 2048)], F32, stack, "junk")
        self.b_junk = Buf("junk")
        self.stat = [self.sb([128, 8], F32, stack, "stat") for _ in range(2)]
        self.b_stat = [Buf("st0"), Buf("st1")]
        self.psT = [self.ps([128, 512], F32, stack, "psT") for _ in range(2)]
        self.b_psT = [Buf("psT0"), Buf("psT1")]
        self.P.chan("h0"); self.P.chan("h1")
        self.ncount = 0

    def norm_tiles(self, src, row0, nrows_list, row, nT, nT_bufs, col0=0, preloaded=False):
        cfg, P = self.cfg, self.P
        D, KC = cfg.D, cfg.KC
        CW = min(D, 2048)
        NCH = D // CW
        r0 = row0
        c0 = col0
        for ti, n in enumerate(nrows_list):
            s = self.ncount % 2
            self.ncount += 1
            hb, bh, stt, bst = self.hbuf[s], self.b_h[s], self.stat[s], self.b_stat[s]
            if not preloaded:
                P.dma("sync", (lambda hb=hb, r0=r0, n=n: lambda e: e.dma_start(out=hb[0:n, :], in_=src[r0:r0 + n, :]))(),
                      "h%d" % s, writes=[bh])
            for c in range(NCH):
                P.op("vector", (lambda hb=hb, c=c, n=n: lambda e: e.tensor_tensor(
                    out=self.junk[0:n, 0:CW], in0=hb[0:n, c * CW:(c + 1) * CW],
                    in1=hb[0:n, c * CW:(c + 1) * CW], op=ALU.mult))(), reads=[bh], writes=[self.b_junk])
                P.op("vector", (lambda stt=stt, c=c, n=n: lambda e: e.tensor_reduce(
                    out=stt[0:n, c:c + 1], in_=self.junk[0:n, 0:CW], axis=AX.X, op=ALU.add))(),
                    reads=[self.b_junk], writes=[bst])
            if NCH > 1:
                P.op("vector", (lambda stt=stt, n=n: lambda e: e.tensor_reduce(
                    out=stt[0:n, 4:5], in_=stt[0:n, 0:NCH], axis=AX.X, op=ALU.add))(), reads=[bst], writes=[bst])
                sscol = 4
            else:
                sscol = 0
            P.op("vector", (lambda stt=stt, n=n, sscol=sscol: lambda e: e.tensor_scalar(
                out=stt[0:n, 5:6], in0=stt[0:n, sscol:sscol + 1], scalar1=1.0 / D, scalar2=EPS,
                op0=ALU.mult, op1=ALU.add))(), reads=[bst], writes=[bst])
            P.op("gpsimd", (lambda stt=stt, n=n: lambda e: e.tensor_tensor(
                out=stt[0:n, 6:7], in0=stt[0:n, 5:6], in1=self.mhalf[0:n, 0:1], op=ALU.pow))(),
                reads=[bst, self.b_const], writes=[bst])
            P.op("scalar", (lambda hb=hb, stt=stt, n=n: lambda e: e.activation(
                out=hb[0:n, :], in_=hb[0:n, :], func=AF.Copy, scale=stt[0:n, 6:7]))(),
                reads=[bh, bst], writes=[bh])
            for kq in range((KC + 3) // 4):
                pb = kq % 2
                ks = list(range(kq * 4, min(KC, kq * 4 + 4)))

                def tr(e, hb=hb, ks=ks, pb=pb, n=n):
                    for q, k in enumerate(ks):
                        ins = e.transpose(out=self.psT[pb][:, q * 128:q * 128 + n],
                                          in_=hb[0:n, k * 128:(k + 1) * 128],
                                          identity=self.ident32[0:n, 0:n])
                    return ins
                P.op("tensor", tr, reads=[bh, self.b_const], writes=[self.b_psT[pb]])
                on_act = (kq % 2 == 1)

                def ev(e, ks=ks, pb=pb, n=n, c0=c0, on_act=on_act):
                    for q, k in enumerate(ks):
                        o = nT[:, k, c0:c0 + n]
                        i = self.psT[pb][:, q * 128:q * 128 + n]
                        if on_act:
                            ins = e.activation(out=o, in_=i, func=AF.Identity,
                                               bias=self.mcol[:, row, 0, k:k + 1],
                                               scale=self.gscol[:, row, k:k + 1])
                        else:
                            ins = e.tensor_scalar(out=o, in0=i, scalar1=self.gscol[:, row, k:k + 1],
                                                  scalar2=self.mcol[:, row, 0, k:k + 1],
                                                  op0=ALU.mult, op1=ALU.add)
                    return ins
                P.op("scalar" if on_act else "vector", ev,
                     reads=[self.b_psT[pb], self.b_const], writes=[nT_bufs[ti]])
            r0 += n
            c0 += n

    def norm_rows(self, pieces, row, nT, nT_buf):
        total = sum(n for _, _, n in pieces)
        s = self.ncount % 2
        hb, bh = self.hbuf[s], self.b_h[s]
        o = 0
        for ap, r0, n in pieces:
            self.P.dma("sync", (lambda hb=hb, ap=ap, r0=r0, n=n, o=o: lambda e: e.dma_start(
                out=hb[o:o + n, :], in_=ap[r0:r0 + n, :]))(), "h%d" % s, writes=[bh])
            o += n
        self.norm_tiles(None, 0, [total], row, nT, [nT_buf], preloaded=True)

    def alloc_wring(self, WT, nslots):
        self.WT = WT
        self.wslot = [self.sb([128, self.cfg.KC, WT], BF16, name="wslot") for _ in range(nslots)]
        self.b_w = [Buf("w%d" % i) for i in range(nslots)]
        for i in range(nslots):
            self.P.chan("w%d" % i)
        self.wsched = []
        self.wloaded = 0
        self.wused = 0

    def _wload_upto(self, idx):
        ns = len(self.wslot)
        while self.wloaded <= idx and self.wloaded < len(self.wsched):
            i = self.wloaded
            s = i % ns
            src = self.wsched[i].rearrange("(kc p) n -> p kc n", p=128)
            self.P.dma("gpsimd", (lambda s=s, src=src: lambda e: e.dma_start(out=self.wslot[s][:], in_=src))(),
                       "w%d" % s, writes=[self.b_w[s]])
            self.wloaded += 1

    def next_w(self):
        i = self.wused
        self.wused += 1
        ns = len(self.wslot)
        self._wload_upto(i + ns - 1)
        return self.wslot[i % ns], self.b_w[i % ns]

    def alloc_tm(self, stack):
        WT = self.WT
        H = WT // 128
        self.psU = [self.ps([128, WT], F32, stack, "psU") for _ in range(2)]
        self.b_psU = [Buf("psU0"), Buf("psU1")]
        self.psB = [self.ps([128, WT], BF16, stack, "psB") for _ in range(2)]
        self.b_psB = [Buf("psB0"), Buf("psB1")]
        self.ybuf = [self.sb([128, H, 128], F32, stack, "ybuf") for _ in range(2)]
        self.b_y = [Buf("y0"), Buf("y1")]
        self.sqb = self.sb([128, H, 128], F32, stack, "sqb")
        self.b_sq = Buf("sq")
        self.st4 = [self.sb([128, 3, H], F32, stack, "st4") for _ in range(2)]
        self.b_st4 = [Buf("st40"), Buf("st41")]
        self.yb = [self.sb([128, H, 128], BF16, stack, "yb") for _ in range(2)]
        self.b_yb = [Buf("yb0"), Buf("yb1")]
        self.rt = [self.sb([128, H, 2, 32], F32, stack, "rt") for _ in range(4)]
        self.b_rt = [Buf("rt%d" % i) for i in range(4)]
        self.tmc = 0
        self.tm_pending = None

    def tm_flush(self):
        if self.tm_pending is not None:
            f, self.tm_pending = self.tm_pending, None
            f()

    def tm_tile(self, wt, bw, nT, nT_buf, col0, mode, norm_bc, rope_tile, dest, dest_bufs):
        cfg, P = self.cfg, self.P
        KC, WT = cfg.KC, self.WT
        H = WT // 128
        i = self.tmc % 2
        self.tmc += 1
        psU, bpsU = self.psU[i], self.b_psU[i]

        def mm(e):
            for k in range(KC):
                ins = e.matmul(psU[:, 0:WT], lhsT=nT[:, k, col0:col0 + 128], rhs=wt[:, k, :],
                               start=(k == 0), stop=(k == KC - 1))
            return ins
        P.op("tensor", mm, reads=[nT_buf, bw], writes=[bpsU])
        self.tm_flush()
        if mode == "plain":
            P.op("scalar", lambda e: e.activation(out=dest, in_=psU[:, 0:WT], func=AF.Copy),
                 reads=[bpsU], writes=dest_bufs)
            return
        y, by = self.ybuf[i], self.b_y[i]
        yf = y[:].rearrange("p h d -> p (h d)")
        P.op("scalar", lambda e: e.activation(out=yf, in_=psU[:, 0:WT], func=AF.Copy), reads=[bpsU], writes=[by])
        if norm_bc is not None:
            s4, bs4 = self.st4[i], self.b_st4[i]
            P.op("vector", lambda e: e.tensor_tensor(out=self.sqb[:], in0=y[:], in1=y[:], op=ALU.mult),
                 reads=[by], writes=[self.b_sq])
            P.op("vector", lambda e: e.tensor_reduce(out=s4[:, 0, :], in_=self.sqb[:], axis=AX.X, op=ALU.add),
                 reads=[self.b_sq], writes=[bs4])
            P.op("vector", lambda e: e.tensor_scalar(out=s4[:, 1, :], in0=s4[:, 0, :], scalar1=1.0 / 128,
                                                     scalar2=EPS, op0=ALU.mult, op1=ALU.add),
                 reads=[bs4], writes=[bs4])
            P.op("gpsimd", lambda e: e.tensor_tensor(out=s4[:, 2, :], in0=s4[:, 1, :], in1=self.mhalf[:, 0:H],
                                                     op=ALU.pow), reads=[bs4, self.b_const], writes=[bs4])
            P.op("vector", lambda e: e.tensor_tensor(
                out=y[:], in0=y[:], in1=s4[:, 2, :].unsqueeze(2).broadcast_to([128, H, 128]), op=ALU.mult),
                reads=[by, bs4], writes=[by])
            P.op("vector", lambda e: e.tensor_tensor(
                out=y[:], in0=y[:], in1=_bc(norm_bc, [[0, H], [1, 128]]), op=ALU.mult),
                reads=[by, self.b_const], writes=[by])
        yb, byb = self.yb[i], self.b_yb[i]
        if rope_tile is not None:
            v5 = y[:].rearrange("p h (a b i) -> p h a b i", a=2, b=2)
            o5 = yb[:].rearrange("p h (a b i) -> p h a b i", a=2, b=2)
            x1, x2 = v5[:, :, :, 0, :], v5[:, :, :, 1, :]
            cosap = _bc(self.cs[:, rope_tile, :], [[0, H], [32, 2], [1, 32]])
            sinap = _bc(self.sn[:, rope_tile, :], [[0, H], [32, 2], [1, 32]])
            r = self.rt
            br = self.b_rt
            P.op("vector", lambda e: e.tensor_tensor(out=r[0][:], in0=x1, in1=cosap, op=ALU.mult),
                 reads=[by, self.b_const], writes=[br[0]])
            P.op("vector", lambda e: e.tensor_tensor(out=r[1][:], in0=x2, in1=sinap, op=ALU.mult),
                 reads=[by, self.b_const], writes=[br[1]])
            P.op("vector", lambda e: e.tensor_tensor(out=r[2][:], in0=x2, in1=cosap, op=ALU.mult),
                 reads=[by, self.b_const], writes=[br[2]])
            P.op("vector", lambda e: e.tensor_tensor(out=r[3][:], in0=x1, in1=sinap, op=ALU.mult),
                 reads=[by, self.b_const], writes=[br[3]])
            P.op("vector", lambda e: e.tensor_tensor(out=o5[:, :, :, 0, :], in0=r[0][:], in1=r[1][:],
                                                     op=ALU.subtract), reads=[br[0], br[1]], writes=[byb])
            P.op("vector", lambda e: e.tensor_tensor(out=o5[:, :, :, 1, :], in0=r[2][:], in1=r[3][:],
                                                     op=ALU.add), reads=[br[2], br[3]], writes=[byb])
        else:
            P.op("vector", lambda e: e.tensor_copy(out=yb[:], in_=y[:]), reads=[by], writes=[byb])
        psB, bpsB = self.psB[i], self.b_psB[i]

        def part2():
            def tr(e):
                for hh in range(H):
                    ins = e.transpose(out=psB[:, hh * 128:(hh + 1) * 128], in_=yb[:, hh, :], identity=self.identb[:])
                return ins
            P.op("tensor", tr, reads=[byb, self.b_const], writes=[bpsB])
            P.op("scalar", lambda e: e.activation(out=dest, in_=psB[:, 0:WT].rearrange("p (h t) -> p h t", h=H),
                                                  func=AF.Copy), reads=[bpsB], writes=dest_bufs)
        self.tm_pending = part2


def token_groups(cfg, GT, with_ctx):
    gs = []
    for g0 in range(0, cfg.TOK, GT):
        n = min(GT, cfg.TOK - g0)
        gs.append((0, g0, [128] * (n // 128)))
    if with_ctx:
        assert cfg.CTX <= GT
        gs.append((1, 0, [128] * (cfg.CTX // 128)))
    return gs


def build_kv(cfg, GT=512, F=None):
    from contextlib import ExitStack
    nc = F.nc if F else bass.Bass("TRN2", target_bir_lowering=False)
    D, KC, TOK, CTX = cfg.D, cfg.KC, cfg.TOK, cfg.CTX
    TA = TOK + CTX
    WT = cfg.WT
    H = WT // 128
    NKA, NKC = cfg.A_KV, 2 * cfg.C_H
    dram = dict(F.dram) if F else {}

    def din(name, shape, dt=F32):
        dram[name] = nc.dram_tensor(name, list(shape), dt, kind="ExternalInput").ap()

    def dout(name, shape, dt):
        dram[name] = nc.dram_tensor(name, list(shape), dt, kind="ExternalOutput").ap()

    if not F:
        din("h", [TOK, D]); din("hc", [CTX, D]); din("ident", [128, 128])
        din("mcol", [128, 2, 3, KC]); din("ng_col", [128, KC])
        din("cs", [TOK, 64]); din("sn", [TOK, 64]); din("kn", [1, 128])
        din("wkv", [D, cfg.KV])
        dout("ktA", [NKA, 128, TA], BF16); dout("vA", [TA, cfg.VA], BF16)
        dout("ktC", [NKC, 128, TA], BF16); dout("vC", [TA, cfg.VC], BF16)

    with ExitStack() as st:
        P = F.P if F else Prog(nc, st)
        L = Layer(cfg, nc, st, P, GT)
        knbc = L.sb([128, 128], F32, name="knbc")

        def extra(dr, bc):
            P.dma("sync", lambda e: e.dma_start(out=knbc[:], in_=dr["kn"].partition_broadcast(128)), "const",
                  writes=[bc])
        L.extra_const_loads = extra
        L.load_consts(dram)
        L.alloc_wring(WT, 2)
        L.alloc_norm(st)
        L.alloc_tm(st)
        nT = L.sb([128, KC, GT], BF16, name="nT")
        ktA = L.sb([128, NKA, GT], BF16, name="ktA")
        ktC = L.sb([128, NKC, GT], BF16, name="ktC")
        b_ktA, b_ktC = Buf("ktA"), Buf("ktC")
        vst = [L.sb([128, WT], BF16, name="vst") for _ in range(2)]
        b_vst = [Buf("vst0"), Buf("vst1")]
        P.chan("vst0"); P.chan("vst1"); P.chan("koutA"); P.chan("koutC")
        if F:
            groups = F.kv_groups
        else:
            groups = [(k, dram["h"] if k == 0 else dram["hc"], g0, tl, (g0 if k == 0 else TOK), g0)
                      for k, g0, tl in token_groups(cfg, GT, True)]
        segs = []
        for c0 in range(0, cfg.KV, WT):
            if c0 < cfg.KA:
                segs.append(("ka", c0))
            elif c0 < cfg.KA + cfg.VA:
                segs.append(("va", c0))
            elif c0 < cfg.KA + cfg.VA + cfg.KCC:
                segs.append(("kc", c0))
            else:
                segs.append(("vc", c0))
        for _ in groups:
            for _, c0 in segs:
                L.wsched.append(dram["wkv"][:, c0:c0 + WT])
        vcount = 0
        nT2 = L.sb([128, KC, GT], BF16, name="nT2")
        nTs = [nT, nT2]
        nbufs_all = [[Buf("nT%d_%d" % (gi, i)) for i in range(len(g[3]))] for gi, g in enumerate(groups)]
        k0_, s0_, r0_, t0_ = groups[0][0], groups[0][1], groups[0][2], groups[0][3]
        L.norm_tiles(s0_, r0_, t0_, k0_, nTs[0], nbufs_all[0])
        for gi, (kind, src, g0, tiles, tok0, rope_row0) in enumerate(groups):
            nT = nTs[gi % 2]
            nbufs = nbufs_all[gi]
            GTg = 128 * len(tiles)
            nxt = groups[gi + 1] if gi + 1 < len(groups) else None
            nxt_done = 0
            if kind == 0:
                L.load_rope(rope_row0, len(tiles))
            for wi, (name, c0) in enumerate(segs):
                wt, bw = L.next_w()
                for ti in range(len(tiles)):
                    t0 = tok0 + ti * 128
                    rope_tile = ti if kind == 0 else None
                    if name == "ka":
                        h0 = c0 // 128
                        L.tm_tile(wt, bw, nT, nbufs[ti], ti * 128, "heads", knbc[:], rope_tile,
                                  ktA[:, h0:h0 + H, ti * 128:(ti + 1) * 128], [b_ktA])
                    elif name == "kc":
                        h0 = (c0 - cfg.KA - cfg.VA) // 128
                        L.tm_tile(wt, bw, nT, nbufs[ti], ti * 128, "heads", None, rope_tile,
                                  ktC[:, h0:h0 + H, ti * 128:(ti + 1) * 128], [b_ktC])
                    else:
                        s = vcount % 2
                        vcount += 1
                        L.tm_tile(wt, bw, nT, nbufs[ti], ti * 128, "plain", None, None, vst[s][:, :], [b_vst[s]])
                        if name == "va":
                            dst = dram["vA"][t0:t0 + 128, c0 - cfg.KA:c0 - cfg.KA + WT]
                        else:
                            cc = c0 - cfg.KA - cfg.VA - cfg.KCC
                            dst = dram["vC"][t0:t0 + 128, cc:cc + WT]
                        P.dma("sync", (lambda dst=dst, s=s: lambda e: e.dma_start(out=dst, in_=vst[s][:, :]))(),
                              "vst%d" % s, reads=[b_vst[s]])
                if nxt is not None:
                    upto = len(nxt[3]) if wi == len(segs) - 1 else min(len(nxt[3]), wi + 1)
                    while nxt_done < upto:
                        L.norm_tiles(nxt[1], nxt[2] + 128 * nxt_done, [nxt[3][nxt_done]], nxt[0],
                                     nTs[(gi + 1) % 2], [nbufs_all[gi + 1][nxt_done]], col0=128 * nxt_done)
                        nxt_done += 1
            L.tm_flush()
            P.dma("sync", (lambda tok0=tok0, GTg=GTg: lambda e: e.dma_start(
                out=dram["ktA"][:, :, tok0:tok0 + GTg].rearrange("h d t -> d h t"), in_=ktA[:, :, 0:GTg]))(),
                "koutA", reads=[b_ktA])
            P.dma("sync", (lambda tok0=tok0, GTg=GTg: lambda e: e.dma_start(
                out=dram["ktC"][:, :, tok0:tok0 + GTg].rearrange("h d t -> d h t"), in_=ktC[:, :, 0:GTg]))(),
                "koutC", reads=[b_ktC])
        P.emit()
    return nc


def rope_tables(cfg):
    n = cfg.SEQ
    gw = 64
    rows = n // gw
    row = np.broadcast_to(np.arange(rows)[:, None], (rows, gw)).reshape(-1).astype(np.float32)
    col = np.broadcast_to(np.arange(gw)[None, :], (rows, gw)).reshape(-1).astype(np.float32)
    inv = (np.float32(10000.0) ** (-np.arange(0, 64, 2, dtype=np.float32) / np.float32(64))).astype(np.float32)
    ar = (row[:, None] * inv).astype(np.float32)
    ac = (col[:, None] * inv).astype(np.float32)
    cs = np.concatenate([np.cos(ar), np.cos(ac)], axis=1).astype(np.float32)
    sn = np.concatenate([np.sin(ar), np.sin(ac)], axis=1).astype(np.float32)
    return cs, sn


def col_layout(v, kc):
    v = np.asarray(v, np.float32)
    lead = v.shape[:-1]
    a = v.reshape(lead + (kc, 128))
    return np.ascontiguousarray(np.moveaxis(a, -1, 0))


def common_inputs(cfg, l, b, j, hs, hcs, m, norm_g, cs, sn):
    mb = m[b][l]
    mcol = col_layout(mb.reshape(2, 3, cfg.D), cfg.KC)
    sl = slice(j * cfg.TOK, (j + 1) * cfg.TOK)
    return {
        "h": np.ascontiguousarray(hs[b][sl]),
        "hc": np.ascontiguousarray(hcs[b]),
        "ident": np.eye(128, dtype=np.float32),
        "mcol": mcol,
        "ng_col": col_layout(norm_g[l], cfg.KC),
        "cs": np.ascontiguousarray(cs[sl]),
        "sn": np.ascontiguousarray(sn[sl]),
    }


def run_kv(cfg, l, hs, hcs, m, norm_g, k_norm_a, w_in, cs, sn, GT=512):
    nc = build_kv(cfg, GT)
    wkv = np.ascontiguousarray(w_in[l][:, :cfg.KV])
    in_maps = []
    for r in range(N_CORES):
        b, j = divmod(r, RANKS)
        d = common_inputs(cfg, l, b, j, hs, hcs, m, norm_g, cs, sn)
        d["kn"] = np.ascontiguousarray(k_norm_a[l].reshape(1, 128)).astype(np.float32)
        d["wkv"] = wkv
        in_maps.append(d)
    res = run_bass_kernel_spmd(nc, in_maps, core_ids=list(range(N_CORES))).results
    out = []
    T = cfg.TOK
    for b in range(cfg.BATCH):
        rs = [res[b * RANKS + j] for j in range(RANKS)]
        kv = {}
        for nm in ("ktA", "ktC"):
            kv[nm] = np.ascontiguousarray(np.concatenate([rs[0][nm][:, :, T:]] + [x[nm][:, :, :T] for x in rs], axis=2))
        for nm in ("vA", "vC"):
            kv[nm] = np.ascontiguousarray(np.concatenate([rs[0][nm][T:]] + [x[nm][:T] for x in rs], axis=0))
        out.append(kv)
    return out


DEBUG = False


def build_rest(cfg, l, last, GT=512, F=None):
    from contextlib import ExitStack
    nc = F.nc if F else bass.Bass("TRN2", target_bir_lowering=False)
    D, KC, TOK, CTX = cfg.D, cfg.KC, cfg.TOK, cfg.CTX
    SK = CTX + cfg.SEQ
    WT = cfg.WT
    HB = WT // 128
    A_H, A_KV, C_H = cfg.A_H, cfg.A_KV, cfg.C_H
    NBC = cfg.B_W // 128
    NCHUNK = D // 128
    NG = (TOK + GT - 1) // GT
    HN = 2 * RANKS + 2 * (NG - 1)
    with_ctx = not last
    lambda_init = 0.8 - 0.6 * math.exp(-0.3 * l)
    QW = cfg.IN - cfg.KV
    o_qa, o_qc = 0, cfg.A_W
    o_xb = o_qc + 2 * 128 * C_H
    o_bb = o_xb + cfg.B_W
    o_cb = o_bb + cfg.B_W
    o_za = o_cb + cfg.B_W
    o_zb = o_za + cfg.A_W
    o_zc = o_zb + cfg.B_W
    assert o_zc + cfg.C_W == QW
    dram = dict(F.dram) if F else {}
    NSEL = F.nsel if F else NG * 2 * HN

    def din(name, shape, dt=F32):
        dram[name] = nc.dram_tensor(name, list(shape), dt, kind="ExternalInput").ap()

    def dout(name, shape, dt):
        dram[name] = nc.dram_tensor(name, list(shape), dt, kind="ExternalOutput").ap()

    if not F:
        din("h", [TOK, D]); din("hc", [CTX, D]); din("ident", [128, 128])
        din("mcol", [128, 2, 3, KC]); din("ng_col", [128, KC])
        din("cs", [TOK, 64]); din("sn", [TOK, 64])
        din("hh", [HN, D]); din("sel", [1, NG * 2 * HN]); din("gate_row", [2, D])
        din("qn", [1, 128]); din("convw", [128, NBC, 3]); din("lam4", [4, 128]); din("sg_col", [128, 2])
        din("wq", [D, QW]); din("wo", [D, D])
        din("ktA", [A_KV, 128, SK], BF16); din("vA", [SK, cfg.VA], BF16)
        din("ktC", [2 * C_H, 128, SK], BF16); din("vC", [SK, cfg.VC], BF16)
        if last:
            din("fg_row", [1, D])
            dram["h2"] = nc.dram_tensor("h2", [TOK, D], F32).ap()
        dout("ho", [TOK, D], F32)
        if with_ctx:
            dout("hco", [CTX, D], F32)
        if DEBUG:
            dout("dbgA", [128, NCHUNK, GT], BF16)
            dout("dbgB", [128, NCHUNK, GT], BF16)
            dout("dbgG", [128, A_H + 2 * C_H, GT], BF16)
    if last:
        h2 = dram["h2"]

    with ExitStack() as st:
        P = F.P if F else Prog(nc, st)
        L = Layer(cfg, nc, st, P, GT)
        qnbc = L.sb([128, 128], F32, name="qnbc")
        convcol = L.sb([128, NBC, 3], F32, name="convcol")
        lamb = L.sb([128, 4, 128], F32, name="lamb")
        lamt = L.sb([128, 2, 128], F32, name="lamt")
        lams = L.sb([128, 8], F32, name="lams")
        sgcol = L.sb([128, 2], F32, name="sgcol")
        selbc = L.sb([128, NSEL], F32, name="selbc")
        onesb = L.sb([128, 128], BF16, name="onesb")
        ones32 = L.sb([128, 128], F32, name="ones32")
        mhalfw = L.sb([128, GT], F32, name="mhalfw")

        def extra(dr, bc):
            def ld(o, i):
                P.dma("sync", lambda e: e.dma_start(out=o, in_=i), "const", writes=[bc])
            ld(qnbc[:], dr["qn"].partition_broadcast(128))
            ld(convcol[:], dr["convw"])
            for i in range(4):
                ld(lamb[:, i, :], dr["lam4"][i:i + 1, :].partition_broadcast(128))
            ld(sgcol[:], dr["sg_col"])
            ld(selbc[:], dr["sel"].partition_broadcast(128))
        L.extra_const_loads = extra
        L.load_consts(dram)
        bc = L.b_const
        P.op("gpsimd", lambda e: e.memset(onesb[:], 1.0), writes=[bc])
        P.op("gpsimd", lambda e: e.memset(ones32[:], 1.0), writes=[bc])
        P.op("gpsimd", lambda e: e.memset(mhalfw[:], -0.5), writes=[bc])
        P.op("vector", lambda e: e.tensor_tensor(
            out=lamt[:], in0=lamb[:].rearrange("p (a b) d -> p a b d", b=2)[:, :, 0, :],
            in1=lamb[:].rearrange("p (a b) d -> p a b d", b=2)[:, :, 1, :], op=ALU.mult), reads=[bc], writes=[bc])
        P.op("vector", lambda e: e.tensor_reduce(out=lams[:, 0:2], in_=lamt[:], axis=AX.X, op=ALU.add),
             reads=[bc], writes=[bc])
        P.op("scalar", lambda e: e.activation(out=lams[:, 2:4], in_=lams[:, 0:2], func=AF.Exp), reads=[bc], writes=[bc])
        P.op("vector", lambda e: e.tensor_tensor(out=lams[:, 4:5], in0=lams[:, 3:4], in1=lams[:, 2:3],
                                                 op=ALU.subtract), reads=[bc], writes=[bc])
        P.op("vector", lambda e: e.tensor_scalar(out=lams[:, 5:6], in0=lams[:, 4:5], scalar1=-lambda_init,
                                                 scalar2=None, op0=ALU.add), reads=[bc], writes=[bc])
        P.op("vector", lambda e: e.tensor_scalar(out=sgcol[:], in0=sgcol[:], scalar1=1.0 - lambda_init,
                                                 scalar2=None, op0=ALU.mult), reads=[bc], writes=[bc])
        neglam = lams[:, 5:6]

        L.alloc_wring(WT, 2)
        mixT = L.sb([128, NCHUNK, GT], BF16, name="mixT")
        b_mix = [Buf("mix%d" % i) for i in range(NCHUNK)]
        NGATE = A_H + 2 * C_H
        gateT = L.sb([128, NGATE, GT], BF16, name="gateT")
        b_gate = [Buf("gate%d" % i) for i in range(NGATE)]
        nTh = L.sb([128, KC, HN], BF16, name="nTh")
        b_nTh = Buf("nTh")
        if last:
            ssq = L.sb([128, TOK // 128, D // WT], F32, name="ssq")
            b_ssq = Buf("ssq")
        if F:
            groups = F.rest_groups
        else:
            groups = []
            for k, g0, tl in token_groups(cfg, GT, with_ctx):
                if k == 0:
                    groups.append((0, dram["h"], g0, tl, (h2 if last else dram["ho"]), g0, g0,
                                   (g0 // GT) * 2 * HN, g0 // 128,
                                   ([(dram["hh"], 0, HN)] if g0 == 0 else None)))
                else:
                    groups.append((1, dram["hc"], 0, tl, dram["hco"], 0, None, None, None, None))

        wq, wo = dram["wq"], dram["wo"]
        per_group = []
        for c0 in range(o_qa, o_qa + cfg.A_W, WT):
            per_group.append(("qa", c0))
        for c0 in range(o_qc, o_qc + 2 * 128 * C_H, WT):
            per_group.append(("qc", c0))
        for mi in range(cfg.B_W // WT):
            for nm, o in (("xb", o_xb), ("cb", o_cb), ("bb", o_bb), ("zb", o_zb)):
                per_group.append((nm, o + mi * WT))
        for c0 in range(o_za, o_za + cfg.A_W, WT):
            per_group.append(("za", c0))
        for c0 in range(o_zc, o_zc + cfg.C_W, WT):
            per_group.append(("zc", c0))
        for _ in groups:
            for nm, c0 in per_group:
                L.wsched.append(wq[:, c0:c0 + WT])
            for c0 in range(0, D, WT):
                L.wsched.append(wo[:, c0:c0 + WT])

        for gi_, (kind, src, g0, tiles, dst, d0, rope_row0, sel_off, ssq_t0, halo_src) in enumerate(groups):
            GTg = 128 * len(tiles)
            if kind == 0:
                L.load_rope(rope_row0, len(tiles))
            with ExitStack() as sA:
                nT = L.sb([128, KC, GT], BF16, sA, "nT")
                nbufs = [Buf("nT%d" % i) for i in range(len(tiles))]
                with ExitStack() as s1:
                    L.alloc_norm(s1)
                    if halo_src is not None:
                        L.norm_rows(halo_src, 0, nTh, b_nTh)
                    L.norm_tiles(src, g0, tiles, kind, nT, nbufs)
                    P.emit("A1")
                with ExitStack() as s2:
                    L.alloc_tm(s2)
                    psF = [L.ps([128, GT], F32, s2, "psF") for _ in range(2)]
                    b_psF = [Buf("psF0"), Buf("psF1")]
                    psH = L.ps([128, 512], F32, s2, "psH")
                    b_psH = Buf("psH")
                    bufX = L.sb([128, HB, GT + 2], F32, s2, "bufX")
                    bufXh = L.sb([128, HB, HN], F32, s2, "bufXh")
                    bufC = L.sb([128, HB, GT], F32, s2, "bufC")
                    bufS = [L.sb([128, GT], F32, s2, "bufS") for _ in range(2)]
                    tmpH = L.sb([128, 2, HN], F32, s2, "tmpH")
                    b_X = [Buf("X%d" % i) for i in range(HB)]
                    b_Xh = [Buf("Xh%d" % i) for i in range(HB)]
                    b_C = [Buf("C%d" % i) for i in range(HB)]
                    b_S = [Buf("S0"), Buf("S1")]
                    b_tH = Buf("tH")
                    fcount = [0]

                    def fm_mm(wt, bw, jb, halo):
                        i = fcount[0] % 2
                        fcount[0] += 1
                        pf, bpf = psF[i], b_psF[i]

                        def mm(e):
                            for k in range(KC):
                                ins = e.matmul(pf[:, 0:GTg], lhsT=wt[:, k, jb * 128:(jb + 1) * 128],
                                               rhs=nT[:, k, 0:GTg], start=(k == 0), stop=(k == KC - 1))
                            return ins
                        P.op("tensor", mm, reads=nbufs + [bw], writes=[bpf])
                        if halo:
                            def mmh(e):
                                for k in range(KC):
                                    ins = e.matmul(psH[:, 0:HN], lhsT=wt[:, k, jb * 128:(jb + 1) * 128],
                                                   rhs=nTh[:, k, :], start=(k == 0), stop=(k == KC - 1))
                                return ins
                            P.op("tensor", mmh, reads=[b_nTh, bw], writes=[b_psH])
                        return pf, bpf

                    scount = 0
                    for nm, c0 in per_group:
                        wt, bw = L.next_w()
                        if nm in ("qa", "qc"):
                            for ti in range(len(tiles)):
                                rope_tile = ti if kind == 0 else None
                                if nm == "qa":
                                    ch0 = (c0 - o_qa) // 128
                                    nb = qnbc[:]
                                else:
                                    ch0 = A_H + NBC + (c0 - o_qc) // 128
                                    nb = None
                                L.tm_tile(wt, bw, nT, nbufs[ti], ti * 128, "heads", nb, rope_tile,
                                          mixT[:, ch0:ch0 + HB, ti * 128:(ti + 1) * 128],
                                          [b_mix[ch0 + q] for q in range(HB)])
                        elif nm in ("za", "zc"):
                            L.tm_flush()
                            g_0 = (c0 - o_za) // 128 if nm == "za" else A_H + (c0 - o_zc) // 128
                            for jb in range(HB):
                                pf, bpf = fm_mm(wt, bw, jb, False)
                                P.op("scalar", (lambda pf=pf, gi=g_0 + jb: lambda e: e.activation(
                                    out=gateT[:, gi, 0:GTg], in_=pf[:, 0:GTg], func=AF.Silu))(),
                                    reads=[bpf], writes=[b_gate[g_0 + jb]])
                        else:
                            L.tm_flush()
                            halo = (kind == 0)
                            blk0 = ((c0 - {"xb": o_xb, "cb": o_cb, "bb": o_bb, "zb": o_zb}[nm]) // 128)
                            for jb in range(HB):
                                cblk = blk0 + jb
                                pf, bpf = fm_mm(wt, bw, jb, halo and nm in ("xb", "cb"))
                                X = bufX[:, jb, 1:GTg + 1]
                                if nm == "xb":
                                    P.op("scalar", (lambda pf=pf, X=X: lambda e: e.activation(
                                        out=X, in_=pf[:, 0:GTg], func=AF.Copy))(), reads=[bpf], writes=[b_X[jb]])
                                    if halo:
                                        P.op("scalar", (lambda jb=jb: lambda e: e.activation(
                                            out=bufXh[:, jb, :], in_=psH[:, 0:HN], func=AF.Copy))(),
                                            reads=[b_psH], writes=[b_Xh[jb]])
                                elif nm == "cb":
                                    P.op("vector", (lambda pf=pf, X=X: lambda e: e.tensor_tensor(
                                        out=X, in0=pf[:, 0:GTg], in1=X, op=ALU.mult))(),
                                        reads=[bpf, b_X[jb]], writes=[b_X[jb]])
                                    pads = _bc(bufX[:, jb, 0:1], [[GTg + 1, 2]])
                                    if halo:
                                        P.op("vector", (lambda jb=jb: lambda e: e.tensor_tensor(
                                            out=bufXh[:, jb, :], in0=psH[:, 0:HN], in1=bufXh[:, jb, :],
                                            op=ALU.mult))(), reads=[b_psH, b_Xh[jb]], writes=[b_Xh[jb]])
                                        selap = selbc[:, sel_off:sel_off + 2 * HN].rearrange(
                                            "p (a h) -> p a h", a=2)
                                        P.op("vector", (lambda jb=jb, selap=selap: lambda e: e.tensor_tensor(
                                            out=tmpH[:], in0=_bc(bufXh[:, jb, :], [[0, 2], [1, HN]]), in1=selap,
                                            op=ALU.mult))(), reads=[b_Xh[jb], bc], writes=[b_tH])
                                        P.op("vector", (lambda pads=pads: lambda e: e.tensor_reduce(
                                            out=pads, in_=tmpH[:], axis=AX.X, op=ALU.add))(),
                                            reads=[b_tH], writes=[b_X[jb]])
                                    else:
                                        P.op("vector", (lambda pads=pads: lambda e: e.memset(pads, 0.0))(),
                                             writes=[b_X[jb]])
                                    Cb = bufC[:, jb, 0:GTg]
                                    P.op("vector", (lambda jb=jb, Cb=Cb, cblk=cblk: lambda e: e.tensor_scalar(
                                        out=Cb, in0=bufX[:, jb, 0:GTg], scalar1=convcol[:, cblk, 0:1], scalar2=None,
                                        op0=ALU.mult))(), reads=[b_X[jb], bc], writes=[b_C[jb]])
                                    for tap in (1, 2):
                                        P.op("vector", (lambda jb=jb, Cb=Cb, cblk=cblk, tap=tap: lambda e:
                                                        e.scalar_tensor_tensor(
                                                            out=Cb, in0=bufX[:, jb, tap:tap + GTg],
                                                            scalar=convcol[:, cblk, tap:tap + 1], in1=Cb,
                                                            op0=ALU.mult, op1=ALU.add))(),
                                             reads=[b_X[jb], b_C[jb], bc], writes=[b_C[jb]])
                                elif nm == "bb":
                                    Cb = bufC[:, jb, 0:GTg]
                                    P.op("vector", (lambda pf=pf, Cb=Cb: lambda e: e.tensor_tensor(
                                        out=Cb, in0=pf[:, 0:GTg], in1=Cb, op=ALU.mult))(),
                                        reads=[bpf, b_C[jb]], writes=[b_C[jb]])
                                else:
                                    si = scount % 2
                                    scount += 1
                                    P.op("scalar", (lambda pf=pf, si=si: lambda e: e.activation(
                                        out=bufS[si][:, 0:GTg], in_=pf[:, 0:GTg], func=AF.Silu))(),
                                        reads=[bpf], writes=[b_S[si]])
                                    P.op("vector", (lambda jb=jb, si=si, cblk=cblk: lambda e: e.tensor_tensor(
                                        out=mixT[:, A_H + cblk, 0:GTg], in0=bufC[:, jb, 0:GTg],
                                        in1=bufS[si][:, 0:GTg], op=ALU.mult))(),
                                        reads=[b_C[jb], b_S[si]], writes=[b_mix[A_H + cblk]])
                    if DEBUG and gi_ == 0:
                        P.chan("dbg")
                        P.dma("sync", lambda e: e.dma_start(out=dram["dbgA"][:, :, :], in_=mixT[:]), "dbg", reads=b_mix)
                        P.dma("sync", lambda e: e.dma_start(out=dram["dbgG"][:, :, :], in_=gateT[:]), "dbg", reads=b_gate)
                    P.emit("A2")

            SKg = SK if kind == 0 else CTX
            NKB = SKg // 128
            with ExitStack() as sB:
                ktb = [L.sb([128, SK], BF16, sB, "ktb") for _ in range(2)]
                NVB = 4
                vtb = [L.sb([128, SK // 128, 128], BF16, sB, "vtb") for _ in range(NVB)]
                b_kt = [Buf("kt0"), Buf("kt1")]
                b_vt = [Buf("vt%d" % i) for i in range(NVB)]
                for nm_ in ["kt0", "kt1"] + ["vt%d" % i for i in range(NVB)]:
                    P.chan(nm_)
                NPB, NSS, LA = 3, 2, 1
                pbuf = [L.sb([128, 2, GT], BF16, sB, "pbuf") for _ in range(NPB)]
                b_p = [Buf("p%d" % i) for i in range(NPB)]
                Layer.UID += 1
                psS = [sB.enter_context(nc.psum_tensor("psS2_%d_%d" % (Layer.UID, i), [128, 1024], F32))
                       for i in range(NSS)]
                b_psS = [Buf("psS%d" % i) for i in range(NSS)]
                psO = [L.ps([128, GT], F32, sB, "psO") for _ in range(2)]
                b_psO = [Buf("psO0"), Buf("psO1")]
                psSums = [L.ps([128, GT], F32, sB, "psSum") for _ in range(2)]
                b_psSums = [Buf("psSum0"), Buf("psSum1")]
                rsb = L.sb([128, GT], F32, sB, "rsb")
                b_rsb = Buf("rsb")
                obuf = [L.sb([128, GT], F32, sB, "obuf") for _ in range(2)]
                b_ob = [Buf("ob0"), Buf("ob1")]
                ocn = [[L.sb([128, GT], F32, sB, "ocn") for _ in range(2)] for _ in range(2)]
                b_ocn = [[Buf("ocn%d%d" % (c, hf)) for hf in range(2)] for c in range(2)]
                dbuf = [L.sb([128, GT], F32, sB, "dbuf") for _ in range(2)]
                b_d = [Buf("d0"), Buf("d1")]
                sqd, b_sqd = ocn[0], b_ocn[0]
                rbuf, b_rb = ocn[1], b_ocn[1]
                cnt = {"kv": 0, "kb": 0, "ob": 0, "unit": 0}

                def attn_unit(kt, bkt, vts, qch):
                    qap = mixT[:, qch, 0:GTg]
                    u = cnt["unit"]
                    cnt["unit"] += 1
                    if len(vts) == 1:
                        outs = [(psO[u % 2], b_psO[u % 2])]
                    else:
                        outs = [(psO[0], b_psO[0]), (psO[1], b_psO[1])]
                    psSum, b_psSum = psSums[u % 2], b_psSums[u % 2]

                    steps = [(k0, min(2, NKB - k0)) for k0 in range(0, NKB, 2)]

                    def S(si, slot):
                        k0, nb = steps[si]

                        def f(e):
                            for b_ in range(nb):
                                ins = e.matmul(psS[slot][:, b_ * 512:b_ * 512 + GTg],
                                               lhsT=kt[:, (k0 + b_) * 128:(k0 + b_ + 1) * 128], rhs=qap,
                                               start=True, stop=True)
                            return ins
                        P.op("tensor", f, reads=[bkt, b_mix[qch]], writes=[b_psS[slot]])
                    base = cnt["kb"]
                    for si in range(min(LA, len(steps))):
                        S(si, (base + si) % NSS)
                    for si, (k0, nb) in enumerate(steps):
                        n_ = base + si
                        if si + LA < len(steps):
                            S(si + LA, (n_ + LA) % NSS)
                        ps_, pp = n_ % NSS, n_ % NPB
                        P.op("scalar", (lambda ps_=ps_, pp=pp, nb=nb: lambda e: e.activation(
                            out=pbuf[pp][:, 0:nb, 0:GTg],
                            in_=psS[ps_][:].rearrange("p (b n) -> p b n", b=2)[:, 0:nb, 0:GTg],
                            func=AF.Exp, scale=ATTN_SCALE))(),
                            reads=[b_psS[ps_]], writes=[b_p[pp]])

                        def pv(e, k0=k0, nb=nb, pp=pp):
                            for b_ in range(nb):
                                kb = k0 + b_
                                for (po, _), (vt, _) in zip(outs, vts):
                                    e.matmul(po[:, 0:GTg], lhsT=vt[:, kb, :], rhs=pbuf[pp][:, b_, 0:GTg],
                                             start=(kb == 0), stop=(kb == NKB - 1))
                                ins = e.matmul(psSum[:, 0:GTg], lhsT=onesb[:], rhs=pbuf[pp][:, b_, 0:GTg],
                                               start=(kb == 0), stop=(kb == NKB - 1))
                            return ins
                        P.op("tensor", pv, reads=[b_p[pp], bc] + [bv for _, bv in vts],
                             writes=[bo for _, bo in outs] + [b_psSum])
                    NKBs = len(steps)
                    cnt["kb"] = base + NKBs
                    P.op("vector", lambda e: e.reciprocal(out=rsb[:, 0:GTg], in_=psSum[:, 0:GTg]),
                         reads=[b_psSum], writes=[b_rsb])
                    return outs

                def load_k(ktsrc, s_):
                    P.dma("sync", lambda e: e.dma_start(out=ktb[s_][:, 0:SKg], in_=ktsrc[:, 0:SKg]), "kt%d" % s_,
                          writes=[b_kt[s_]])

                def load_v(vsrc, s_):
                    P.dma("sync", lambda e: e.dma_start(
                        out=vtb[s_][:, 0:NKB, :],
                        in_=vsrc.rearrange("(kb p) e -> p kb e", p=128)[:, 0:NKB, :]), "vt%d" % s_,
                        writes=[b_vt[s_]])

                segs_b = []
                vn = 0
                for g in range(A_KV):
                    segs_b.append({"k": dram["ktA"][g], "v": [(dram["vA"][:, g * 128:(g + 1) * 128], vn % NVB)],
                                   "vb": [vn % NVB], "kind": "A", "g": g})
                    vn += 1
                for hc in range(C_H):
                    vb = [vn % NVB, (vn + 1) % NVB]
                    vn += 2
                    for c in range(2):
                        segs_b.append({"k": dram["ktC"][2 * hc + c],
                                       "v": ([(dram["vC"][:, hc * 256 + hf * 128:hc * 256 + (hf + 1) * 128], vb[hf])
                                              for hf in range(2)] if c == 0 else []),
                                       "vb": vb, "kind": "C", "hc": hc, "c": c})

                def seg_load(si):
                    sg = segs_b[si]
                    load_k(sg["k"], si % 2)
                    for vsrc, bi in sg["v"]:
                        load_v(vsrc, bi)
                seg_load(0)


                for g in range(A_KV):
                    s_ = g % 2
                    sv = segs_b[g]["vb"][0]
                    if g + 1 < len(segs_b):
                        seg_load(g + 1)
                    for q in range(4):
                        hq = g * 4 + q
                        (po, bpo), = attn_unit(ktb[s_], b_kt[s_], [(vtb[sv], b_vt[sv])], hq)
                        oi = cnt["ob"] % 2
                        cnt["ob"] += 1
                        P.op("vector", (lambda oi=oi, po=po: lambda e: e.tensor_tensor(
                            out=obuf[oi][:, 0:GTg], in0=po[:, 0:GTg], in1=rsb[:, 0:GTg], op=ALU.mult))(),
                            reads=[bpo, b_rsb], writes=[b_ob[oi]])
                        P.op("vector", (lambda oi=oi, hq=hq: lambda e: e.tensor_tensor(
                            out=mixT[:, hq, 0:GTg], in0=obuf[oi][:, 0:GTg], in1=gateT[:, hq, 0:GTg],
                            op=ALU.mult))(), reads=[b_ob[oi], b_gate[hq]], writes=[b_mix[hq]])
                for hc in range(C_H):
                    ch0 = A_H + NBC + 2 * hc
                    for c in range(2):
                        si = A_KV + 2 * hc + c
                        s_ = si % 2
                        vb = segs_b[si]["vb"]
                        if si + 1 < len(segs_b):
                            seg_load(si + 1)
                        attn_unit(ktb[s_], b_kt[s_], [(vtb[vb[0]], b_vt[vb[0]]), (vtb[vb[1]], b_vt[vb[1]])], ch0 + c)
                        for hf in range(2):
                            P.op("vector", (lambda c=c, hf=hf: lambda e: e.tensor_tensor(
                                out=ocn[c][hf][:, 0:GTg], in0=psO[hf][:, 0:GTg], in1=rsb[:, 0:GTg],
                                op=ALU.mult))(), reads=[b_psO[hf], b_rsb], writes=[b_ocn[c][hf]])
                    for hf in range(2):
                        P.op("vector", (lambda hf=hf: lambda e: e.scalar_tensor_tensor(
                            out=dbuf[hf][:, 0:GTg], in0=ocn[1][hf][:, 0:GTg], scalar=neglam,
                            in1=ocn[0][hf][:, 0:GTg], op0=ALU.mult, op1=ALU.add))(),
                            reads=[b_ocn[1][hf], b_ocn[0][hf], bc], writes=[b_d[hf]])
                        P.op("vector", (lambda hf=hf: lambda e: e.tensor_tensor(
                            out=sqd[hf][:, 0:GTg], in0=dbuf[hf][:, 0:GTg], in1=dbuf[hf][:, 0:GTg],
                            op=ALU.mult))(), reads=[b_d[hf]], writes=[b_sqd[hf]])

                    rs_ = cnt["kb"] % NSS
                    cnt["kb"] += 1
                    psR, b_psR = psS[rs_], b_psS[rs_]

                    def rmm(e, psR=psR):
                        e.matmul(psR[:, 0:GTg], lhsT=ones32[:], rhs=sqd[0][:, 0:GTg], start=True, stop=False)
                        return e.matmul(psR[:, 0:GTg], lhsT=ones32[:], rhs=sqd[1][:, 0:GTg], start=False, stop=True)
                    P.op("tensor", rmm, reads=[b_sqd[0], b_sqd[1], bc], writes=[b_psR])
                    P.op("vector", (lambda psR=psR: lambda e: e.tensor_scalar(
                        out=rbuf[0][:, 0:GTg], in0=psR[:, 0:GTg], scalar1=1.0 / 256, scalar2=EPS, op0=ALU.mult,
                        op1=ALU.add))(), reads=[b_psR], writes=[b_rb[0]])
                    P.op("scalar", lambda e: e.activation(out=rbuf[0][:, 0:GTg], in_=rbuf[0][:, 0:GTg],
                                                          func=AF.Sqrt), reads=[b_rb[0]], writes=[b_rb[0]])
                    P.op("vector", lambda e: e.reciprocal(out=rbuf[1][:, 0:GTg], in_=rbuf[0][:, 0:GTg]),
                         reads=[b_rb[0]], writes=[b_rb[1]])
                    for hf in range(2):
                        oi = cnt["ob"] % 2
                        cnt["ob"] += 1
                        P.op("vector", (lambda hf=hf, oi=oi: lambda e: e.scalar_tensor_tensor(
                            out=obuf[oi][:, 0:GTg], in0=dbuf[hf][:, 0:GTg], scalar=sgcol[:, hf:hf + 1],
                            in1=rbuf[1][:, 0:GTg], op0=ALU.mult, op1=ALU.mult))(),
                            reads=[b_d[hf], b_rb[1], bc], writes=[b_ob[oi]])
                        P.op("vector", (lambda hf=hf, oi=oi, hc=hc: lambda e: e.tensor_tensor(
                            out=mixT[:, A_H + NBC + 2 * hc + hf, 0:GTg], in0=obuf[oi][:, 0:GTg],
                            in1=gateT[:, A_H + 2 * hc + hf, 0:GTg], op=ALU.mult))(),
                            reads=[b_ob[oi], b_gate[A_H + 2 * hc + hf]], writes=[b_mix[A_H + NBC + 2 * hc + hf]])
                if DEBUG and gi_ == 0:
                    P.dma("sync", lambda e: e.dma_start(out=dram["dbgB"][:, :, :], in_=mixT[:]), "dbg", reads=b_mix)
                P.emit("B")

            with ExitStack() as sC:
                gbc = L.sb([128, D], F32, sC, "gbc")
                b_gbc = Buf("gbc")
                P.chan("gbc")
                hres = [L.sb([128, WT], F32, sC, "hres") for _ in range(2)]
                b_hres = [Buf("hres0"), Buf("hres1")]
                ores = [L.sb([128, WT], F32, sC, "ores") for _ in range(2)]
                b_ores = [Buf("ores0"), Buf("ores1")]
                psC = [L.ps([128, WT], F32, sC, "psC") for _ in range(2)]
                b_psC = [Buf("psC0"), Buf("psC1")]
                junkc = L.sb([128, WT], F32, sC, "junkc")
                b_junkc = Buf("junkc")
                for nm_ in ("hres0", "hres1", "ores0", "ores1"):
                    P.chan(nm_)
                P.dma("sync", lambda e: e.dma_start(out=gbc[:], in_=dram["gate_row"][kind:kind + 1, :]
                                                    .partition_broadcast(128)), "gbc", writes=[b_gbc])
                ccount = 0
                for c0 in range(0, D, WT):
                    wt, bw = L.next_w()
                    for ti in range(len(tiles)):
                        i = ccount % 2
                        ccount += 1
                        r0 = g0 + ti * 128
                        rd = d0 + ti * 128

                        def mm(e, wt=wt, ti=ti, i=i):
                            for k in range(NCHUNK):
                                ins = e.matmul(psC[i][:, 0:WT], lhsT=mixT[:, k, ti * 128:(ti + 1) * 128],
                                               rhs=wt[:, k, :], start=(k == 0), stop=(k == NCHUNK - 1))
                            return ins
                        P.op("tensor", mm, reads=b_mix + [bw], writes=[b_psC[i]])
                        P.dma("sync", (lambda i=i, r0=r0, c0=c0: lambda e: e.dma_start(
                            out=hres[i][:, :], in_=src[r0:r0 + 128, c0:c0 + WT]))(), "hres%d" % i,
                            writes=[b_hres[i]])
                        P.op("vector", (lambda i=i, c0=c0: lambda e: e.tensor_tensor(
                            out=ores[i][:, :], in0=psC[i][:, 0:WT], in1=gbc[:, c0:c0 + WT], op=ALU.mult))(),
                            reads=[b_psC[i], b_gbc], writes=[b_ores[i]])
                        P.op("vector", (lambda i=i: lambda e: e.tensor_tensor(
                            out=ores[i][:, :], in0=ores[i][:, :], in1=hres[i][:, :], op=ALU.add))(),
                            reads=[b_ores[i], b_hres[i]], writes=[b_ores[i]])
                        if last:
                            tg = ssq_t0 + ti
                            P.op("vector", (lambda i=i: lambda e: e.tensor_tensor(
                                out=junkc[:, :], in0=ores[i][:, :], in1=ores[i][:, :], op=ALU.mult))(),
                                reads=[b_ores[i]], writes=[b_junkc])
                            P.op("vector", (lambda tg=tg, c0=c0: lambda e: e.tensor_reduce(
                                out=ssq[:, tg, c0 // WT:c0 // WT + 1], in_=junkc[:, :], axis=AX.X, op=ALU.add))(),
                                reads=[b_junkc], writes=[b_ssq])
                        P.dma("sync", (lambda i=i, rd=rd, c0=c0: lambda e: e.dma_start(
                            out=dst[rd:rd + 128, c0:c0 + WT], in_=ores[i][:, :]))(), "ores%d" % i,
                            reads=[b_ores[i]])
                P.emit("C")

        if last:
            with ExitStack() as sF:
                fgb = L.sb([128, D], F32, sF, "fgb")
                b_fgb = Buf("fgb")
                P.chan("fgb"); P.chan("f0"); P.chan("f1"); P.chan("fo0"); P.chan("fo1")
                fb = [L.sb([128, D], F32, sF, "fb") for _ in range(2)]
                b_fb = [Buf("fb0"), Buf("fb1")]
                fst = L.sb([128, TOK // 128, 4], F32, sF, "fst")
                b_fst = Buf("fst")
                P.dma("sync", lambda e: e.dma_start(out=fgb[:], in_=dram["fg_row"].partition_broadcast(128)),
                      "fgb", writes=[b_fgb])
                NTL = TOK // 128
                P.op("vector", lambda e: e.tensor_reduce(out=fst[:, :, 0], in_=ssq[:], axis=AX.X, op=ALU.add),
                     reads=[b_ssq], writes=[b_fst])
                P.op("vector", lambda e: e.tensor_scalar(out=fst[:, :, 1], in0=fst[:, :, 0], scalar1=1.0 / D,
                                                         scalar2=EPS, op0=ALU.mult, op1=ALU.add),
                     reads=[b_fst], writes=[b_fst])
                P.op("gpsimd", lambda e: e.tensor_tensor(out=fst[:, :, 2], in0=fst[:, :, 1],
                                                         in1=mhalfw[:, 0:NTL], op=ALU.pow),
                     reads=[b_fst, bc], writes=[b_fst])
                for t in range(NTL):
                    i = t % 2
                    P.dma("sync", (lambda i=i, t=t: lambda e: e.dma_start(
                        out=fb[i][:], in_=h2[t * 128:(t + 1) * 128, :]))(), "f%d" % i, writes=[b_fb[i]])
                    P.op("vector", (lambda i=i, t=t: lambda e: e.scalar_tensor_tensor(
                        out=fb[i][:], in0=fb[i][:], scalar=fst[:, t, 2:3], in1=fgb[:], op0=ALU.mult,
                        op1=ALU.mult))(), reads=[b_fb[i], b_fst, b_fgb], writes=[b_fb[i]])
                    P.dma("sync", (lambda i=i, t=t: lambda e: e.dma_start(
                        out=dram["ho"][t * 128:(t + 1) * 128, :], in_=fb[i][:]))(), "fo%d" % i, reads=[b_fb[i]])
                P.emit("final")
    return nc


class _F:
    pass


def build_fused(cfg, GT=512):
    from contextlib import ExitStack
    nc = bass.Bass("TRN2", target_bir_lowering=False)
    D, KC, TOK, CTX, SEQ = cfg.D, cfg.KC, cfg.TOK, cfg.CTX, cfg.SEQ
    SK = CTX + SEQ
    LYR = cfg.DEPTH
    NS = RANKS
    NG = TOK // GT
    assert NG * GT == TOK
    HN = 2 * RANKS + 2 * (NG - 1)
    NBC = cfg.B_W // 128
    A_KV, C_H = cfg.A_KV, cfg.C_H
    WM = 512
    dram = {}

    def din(name, shape, dt=F32):
        dram[name] = nc.dram_tensor(name, list(shape), dt, kind="ExternalInput").ap()

    def scratch(name, shape, dt=F32):
        dram[name] = nc.dram_tensor(name, list(shape), dt).ap()

    din("x", [SEQ, D]); din("ctx", [CTX, D]); din("ident", [128, 128]); din("cT", [128, KC, 2])
    din("wm", [LYR, D, 3 * D]); din("bm_col", [128, LYR, 3 * KC]); din("ng_col", [128, LYR, KC])
    din("kn", [LYR, 128]); din("qn", [LYR, 128]); din("convw", [128, LYR, NBC, 3])
    din("lam4", [LYR, 4, 128]); din("sg_col", [128, LYR, 2])
    din("cs", [SEQ, 64]); din("sn", [SEQ, 64]); din("sel", [1, NS * NG * 2 * HN]); din("fg_row", [1, D])
    din("w_in", [LYR, D, cfg.IN]); din("w_out", [LYR, D, D])
    dram["out"] = nc.dram_tensor("out", [TOK, D], F32, kind="ExternalOutput").ap()
    scratch("mcol_d", [LYR, 128, 2, 3, KC]); scratch("grow_d", [LYR, 2, D])
    scratch("ktA_d", [A_KV, 128, SK], BF16); scratch("vA_d", [SK, cfg.VA], BF16)
    scratch("ktC_d", [2 * C_H, 128, SK], BF16); scratch("vC_d", [SK, cfg.VC], BF16)
    scratch("h1_d", [SEQ, D]); scratch("hc1_d", [CTX, D]); scratch("h2_d", [TOK, D])

    with ExitStack() as st:
        P = Prog(nc, st)
        with ExitStack() as sm:
            Lm = Layer(cfg, nc, sm, P, GT)
            c32 = Lm.sb([128, KC, 2], F32, name="c32")
            cb = Lm.sb([128, KC, 2], BF16, name="cb")
            bmcol = Lm.sb([128, LYR, 3 * KC], F32, name="bmcol")
            id32 = Lm.sb([128, 128], F32, name="id32")
            mcolT = [Lm.sb([128, 2, 3, KC], F32, name="mcolT") for _ in range(LYR)]
            growT = Lm.sb([KC, 2, 128], F32, name="growT")
            b_c, b_m, b_g = Buf("c"), [Buf("m%d" % l) for l in range(LYR)], Buf("growT")
            psM = [Lm.ps([128, 8], F32, sm, "psM") for _ in range(2)]
            b_psM = [Buf("psM0"), Buf("psM1")]
            psG = Lm.ps([128, 128], F32, sm, "psG")
            b_psG = Buf("psG")
            P.chan("mc"); P.chan("mo")
            for o, i in ((c32[:], dram["cT"]), (bmcol[:], dram["bm_col"]), (id32[:], dram["ident"])):
                P.dma("sync", (lambda o=o, i=i: lambda e: e.dma_start(out=o, in_=i))(), "mc", writes=[b_c])
            P.op("scalar", lambda e: e.activation(out=cb[:], in_=c32[:], func=AF.Silu), reads=[b_c], writes=[b_c])
            Lm.alloc_wring(WM, 2)
            for l in range(LYR):
                for c0 in range(0, 3 * D, WM):
                    Lm.wsched.append(dram["wm"][l][:, c0:c0 + WM])
            tcount = 0
            for l in range(LYR):
                for c0 in range(0, 3 * D, WM):
                    wt, bw = Lm.next_w()
                    i = tcount % 2
                    tcount += 1

                    def mm(e, wt=wt, i=i):
                        for blk in range(WM // 128):
                            for k in range(KC):
                                ins = e.matmul(psM[i][:, blk * 2:blk * 2 + 2], lhsT=wt[:, k, blk * 128:(blk + 1) * 128],
                                               rhs=cb[:, k, :], start=(k == 0), stop=(k == KC - 1))
                        return ins
                    P.op("tensor", mm, reads=[bw, b_c], writes=[b_psM[i]])

                    def ev(e, l=l, c0=c0, i=i):
                        for blk in range(WM // 128):
                            cblk = c0 // 128 + blk
                            ins = e.tensor_scalar(out=mcolT[l][:, :, cblk // KC, cblk % KC],
                                                  in0=psM[i][:, blk * 2:blk * 2 + 2],
                                                  scalar1=bmcol[:, l, cblk:cblk + 1], scalar2=None, op0=ALU.add)
                        return ins
                    P.op("vector", ev, reads=[b_psM[i], b_c], writes=[b_m[l]])
                P.dma("sync", (lambda l=l: lambda e: e.dma_start(out=dram["mcol_d"][l], in_=mcolT[l][:]))(), "mo",
                      reads=[b_m[l]])
                for r in range(2):
                    P.op("tensor", (lambda l=l, r=r: lambda e: e.transpose(
                        out=psG[0:KC, 0:128], in_=mcolT[l][:, r, 2, :], identity=id32[:]))(),
                        reads=[b_m[l], b_c], writes=[b_psG])
                    P.op("scalar", (lambda r=r: lambda e: e.activation(out=growT[:, r, :], in_=psG[0:KC, 0:128],
                                                                       func=AF.Copy))(), reads=[b_psG], writes=[b_g])
                P.dma("sync", (lambda l=l: lambda e: e.dma_start(
                    out=dram["grow_d"][l].rearrange("r (k p) -> k r p", p=128), in_=growT[:]))(), "mo", reads=[b_g])
            P.emit()

        tiles_ctx = [128] * (CTX // 128)
        tiles_g = [128] * (GT // 128)
        for l in range(LYR):
            last = (l == LYR - 1)
            hsrc = dram["x"] if l == 0 else dram["h1_d"]
            hcsrc = dram["ctx"] if l == 0 else dram["hc1_d"]
            F = _F()
            F.nc, F.P = nc, P
            F.nsel = NS * NG * 2 * HN
            F.dram = {
                "ident": dram["ident"], "mcol": dram["mcol_d"][l], "ng_col": dram["ng_col"][:, l, :],
                "cs": dram["cs"], "sn": dram["sn"], "kn": dram["kn"][l:l + 1, :],
                "wkv": dram["w_in"][l][:, 0:cfg.KV],
                "ktA": dram["ktA_d"], "vA": dram["vA_d"], "ktC": dram["ktC_d"], "vC": dram["vC_d"],
                "gate_row": dram["grow_d"][l], "qn": dram["qn"][l:l + 1, :], "convw": dram["convw"][:, l],
                "lam4": dram["lam4"][l], "sg_col": dram["sg_col"][:, l, :], "sel": dram["sel"],
                "wq": dram["w_in"][l][:, cfg.KV:cfg.IN], "wo": dram["w_out"][l],
                "h2": dram["h2_d"], "ho": dram["out"], "fg_row": dram["fg_row"],
            }
            F.kv_groups = [(1, hcsrc, 0, tiles_ctx, 0, None)]
            for i in range(NS):
                for g in range(NG):
                    r0 = i * TOK + g * GT
                    F.kv_groups.append((0, hsrc, r0, tiles_g, CTX + r0, r0))
            build_kv(cfg, GT, F)
            F.rest_groups = []
            for i in range(NS if not last else 1):
                for g in range(NG):
                    r0 = i * TOK + g * GT
                    halo = None
                    if g == 0:
                        halo = []
                        for r in range(RANKS):
                            halo += [(hsrc, r * TOK, 1), (hsrc, r * TOK + TOK - 1, 1)]
                        for gg in range(1, NG):
                            halo.append((hsrc, i * TOK + gg * GT - 1, 2))
                    dst, d0 = (dram["h2_d"], g * GT) if last else (dram["h1_d"], r0)
                    F.rest_groups.append((0, hsrc, r0, tiles_g, dst, d0, r0, (i * NG + g) * 2 * HN,
                                          (g * GT) // 128, halo))
            if not last:
                F.rest_groups.append((1, hcsrc, 0, tiles_ctx, dram["hc1_d"], 0, None, None, None, None))
            build_rest(cfg, l, last, GT, F)
    return nc


def run_fused(cfg, I, GT=512):
    nc = build_fused(cfg, GT)
    D, KC, TOK = cfg.D, cfg.KC, cfg.TOK
    NG = TOK // GT
    HN = 2 * RANKS + 2 * (NG - 1)
    NBC = cfg.B_W // 128
    LYR = cfg.DEPTH
    cs, sn = rope_tables(cfg)
    shared = {
        "ident": np.eye(128, dtype=np.float32),
        "wm": np.ascontiguousarray(I["w_mod"], dtype=np.float32),
        "bm_col": np.ascontiguousarray(col_layout(I["b_mod"], 3 * KC)),
        "ng_col": np.ascontiguousarray(col_layout(I["norm_g"], KC)),
        "kn": np.ascontiguousarray(I["k_norm_a"], dtype=np.float32),
        "qn": np.ascontiguousarray(I["q_norm_a"], dtype=np.float32),
        "convw": np.ascontiguousarray(
            np.transpose(I["conv_w"], (0, 2, 1)).reshape(LYR, NBC, 128, 3).transpose(2, 0, 1, 3), dtype=np.float32),
        "lam4": np.ascontiguousarray(np.stack([I["lambda_q1"], I["lambda_k1"], I["lambda_q2"], I["lambda_k2"]], 1),
                                     dtype=np.float32),
        "sg_col": np.ascontiguousarray(I["subln_g"].reshape(LYR, 2, 128).transpose(2, 0, 1), dtype=np.float32),
        "fg_row": np.ascontiguousarray(I["final_g"].reshape(1, D), dtype=np.float32),
        "w_in": np.ascontiguousarray(I["w_in"], dtype=np.float32),
        "w_out": np.ascontiguousarray(I["w_out"], dtype=np.float32),
    }
    in_maps = []
    for r in range(N_CORES):
        b, j = divmod(r, RANKS)
        order = [(j + i) % RANKS for i in range(RANKS)]
        idx = np.concatenate([np.arange(o * TOK, (o + 1) * TOK) for o in order])
        rows = np.stack([I["c"][b], I["c_ctx"]], axis=-1)
        sel = np.zeros((RANKS, NG, 2, HN), np.float32)
        for i in range(RANKS):
            rk = order[i]
            for g in range(NG):
                if g == 0:
                    if rk > 0:
                        sel[i, g, 0, 2 * ((i - 1) % RANKS) + 1] = 1.0
                else:
                    sel[i, g, 0, 2 * RANKS + 2 * (g - 1)] = 1.0
                if g == NG - 1:
                    if rk < RANKS - 1:
                        sel[i, g, 1, 2 * ((i + 1) % RANKS)] = 1.0
                else:
                    sel[i, g, 1, 2 * RANKS + 2 * g + 1] = 1.0
        d = dict(shared)
        d.update({
            "x": np.ascontiguousarray(I["x"][b][idx]),
            "ctx": np.ascontiguousarray(I["ctx"][b]),
            "cT": np.ascontiguousarray(rows.reshape(KC, 128, 2).transpose(1, 0, 2)).astype(np.float32),
            "cs": np.ascontiguousarray(cs[idx]), "sn": np.ascontiguousarray(sn[idx]),
            "sel": sel.reshape(1, -1),
        })
        in_maps.append(d)
    res = run_bass_kernel_spmd(nc, in_maps, core_ids=list(range(N_CORES))).results
    out = np.empty((cfg.BATCH, cfg.SEQ, D), np.float32)
    for r in range(N_CORES):
        b, j = divmod(r, RANKS)
        out[b, j * TOK:(j + 1) * TOK] = res[r]["out"]
    return out


def halo_rows(cfg, hb, j, GT):
    T = cfg.TOK
    rows = []
    for r in range(RANKS):
        rows += [hb[r * T], hb[r * T + T - 1]]
    NG = (T + GT - 1) // GT
    for g in range(1, NG):
        rows += [hb[j * T + g * GT - 1], hb[j * T + g * GT]]
    return np.ascontiguousarray(np.stack(rows, 0)).astype(np.float32)


def sel_vectors(cfg, j, GT):
    T = cfg.TOK
    NG = (T + GT - 1) // GT
    HN = 2 * RANKS + 2 * (NG - 1)
    sel = np.zeros((NG, 2, HN), np.float32)
    for g in range(NG):
        if g == 0:
            if j > 0:
                sel[g, 0, 2 * (j - 1) + 1] = 1.0
        else:
            sel[g, 0, 2 * RANKS + 2 * (g - 1)] = 1.0
        if g == NG - 1:
            if j < RANKS - 1:
                sel[g, 1, 2 * (j + 1)] = 1.0
        else:
            sel[g, 1, 2 * RANKS + 2 * g + 1] = 1.0
    return sel.reshape(1, -1)


def run_rest(cfg, l, last, hs, hcs, m, kv, P_, cs, sn, GT=512):
    nc = build_rest(cfg, l, last, GT)
    D = cfg.D
    wq = np.ascontiguousarray(P_["w_in"][l][:, cfg.KV:])
    wo = np.ascontiguousarray(P_["w_out"][l])
    NBC = cfg.B_W // 128
    convw = np.ascontiguousarray(P_["conv_w"][l].T.reshape(NBC, 128, 3).transpose(1, 0, 2)).astype(np.float32)
    lam4 = np.stack([P_["lambda_q1"][l], P_["lambda_k1"][l], P_["lambda_q2"][l], P_["lambda_k2"][l]], 0)
    sg_col = np.ascontiguousarray(P_["subln_g"][l].reshape(2, 128).T).astype(np.float32)
    in_maps = []
    for r in range(N_CORES):
        b, j = divmod(r, RANKS)
        d = common_inputs(cfg, l, b, j, hs, hcs, m, P_["norm_g"], cs, sn)
        d["hh"] = halo_rows(cfg, hs[b], j, GT)
        d["sel"] = sel_vectors(cfg, j, GT)
        d["gate_row"] = np.ascontiguousarray(m[b][l][:, 2 * D:3 * D]).astype(np.float32)
        d["qn"] = np.ascontiguousarray(P_["q_norm_a"][l].reshape(1, 128)).astype(np.float32)
        d["convw"] = convw
        d["lam4"] = np.ascontiguousarray(lam4).astype(np.float32)
        d["sg_col"] = sg_col
        d["wq"] = wq
        d["wo"] = wo
        d.update(kv[b])
        if last:
            d["fg_row"] = np.ascontiguousarray(P_["final_g"].reshape(1, D)).astype(np.float32)
        in_maps.append(d)
    res = run_bass_kernel_spmd(nc, in_maps, core_ids=list(range(N_CORES))).results
    if DEBUG:
        run_rest.dbg = res
    hs2 = [np.concatenate([res[b * RANKS + j]["ho"] for j in range(RANKS)], 0) for b in range(cfg.BATCH)]
    hcs2 = None if last else [res[b * RANKS]["hco"] for b in range(cfg.BATCH)]
    return hs2, hcs2


def forward(cfg, inputs, GT=512):
    I = {k: np.asarray(v) for k, v in inputs.items()}
    cs, sn = rope_tables(cfg)
    m = run_mod(cfg, I["c"], I["c_ctx"], I["w_mod"], I["b_mod"])
    hs = [I["x"][b] for b in range(cfg.BATCH)]
    hcs = [I["ctx"][b] for b in range(cfg.BATCH)]
    for l in range(cfg.DEPTH):
        last = (l == cfg.DEPTH - 1)
        kv = run_kv(cfg, l, hs, hcs, m, I["norm_g"], I["k_norm_a"], I["w_in"], cs, sn, GT)
        hs, hcs2 = run_rest(cfg, l, last, hs, hcs, m, kv, I, cs, sn, GT)
        if not last:
            hcs = hcs2
    return np.stack(hs, 0).astype(np.float32)


MODE = "unfused"


def kernel(**inputs):
    cfg = Cfg()
    if MODE == "fused":
        return run_fused(cfg, {k: np.asarray(v) for k, v in inputs.items()})
    return forward(cfg, inputs)
```
